# Optimizing a Trainium2 kernel written in Bass

```python
import math
import jax, jax.numpy as jnp
from jax import lax
import numpy as np

D_MODEL = 1024
BATCH = 4
SEQ = 4096
DEPTH = 2

N_MIXERS = 2
MLSTM_HEADS = 4
MLSTM_DQK = D_MODEL // (2 * MLSTM_HEADS)
MLSTM_DV = D_MODEL // MLSTM_HEADS
MLSTM_QK = MLSTM_HEADS * MLSTM_DQK
MLSTM_V = MLSTM_HEADS * MLSTM_DV
MLSTM_IN = 2 * MLSTM_QK + 2 * MLSTM_V + 2 * MLSTM_HEADS
CHUNK = 64
SCONV_WIDTH = D_MODEL
CONV_K = 3
D_FF = 2816
EPS = 1e-6

kernel_name = "xlstm_shortconv_convffn_hybrid"


def rmsnorm(x, g):
    xf = x.astype(jnp.float32)
    y = xf * lax.rsqrt(jnp.mean(xf * xf, axis=-1, keepdims=True) + EPS) * g.astype(jnp.float32)
    return y.astype(x.dtype)


def causal_dwconv(x, w, b):
    c = x.shape[-1]
    rhs = w.astype(x.dtype)[:, None, :]
    y = lax.conv_general_dilated(
        x, rhs, window_strides=(1,), padding=[(CONV_K - 1, 0)],
        dimension_numbers=('NWC', 'WIO', 'NWC'), feature_group_count=c)
    return y + b.astype(x.dtype)


def mlstm_mixer(x, w_in, b_gates, head_norm, w_out):
    bsz, s, _ = x.shape
    h_, dqk, dv, L = MLSTM_HEADS, MLSTM_DQK, MLSTM_DV, CHUNK
    nc = s // L
    proj = x @ w_in
    i0 = MLSTM_QK
    i1 = i0 + MLSTM_QK
    i2 = i1 + MLSTM_V
    i3 = i2 + MLSTM_V
    i4 = i3 + MLSTM_HEADS
    q = proj[..., :i0].astype(jnp.float32) * (dqk ** -0.5)
    k = proj[..., i0:i1].astype(jnp.float32)
    v = proj[..., i1:i2].astype(jnp.float32)
    o_gate = jax.nn.sigmoid(proj[..., i2:i3].astype(jnp.float32))
    bg = b_gates.astype(jnp.float32)
    log_i = proj[..., i3:i4].astype(jnp.float32) + bg[:MLSTM_HEADS]
    log_f = jax.nn.log_sigmoid(proj[..., i4:].astype(jnp.float32) + bg[MLSTM_HEADS:])

    def to_chunks(t, d):
        return t.reshape(bsz, nc, L, h_, d).transpose(1, 0, 3, 2, 4)

    qc = to_chunks(q, dqk)
    kc = to_chunks(k, dqk)
    vc = to_chunks(v, dv)
    lic = log_i.reshape(bsz, nc, L, h_).transpose(1, 0, 3, 2)
    lfc = log_f.reshape(bsz, nc, L, h_).transpose(1, 0, 3, 2)
    causal = jnp.tril(jnp.ones((L, L), dtype=bool))

    def step(carry, xs):
        C, n, m = carry
        qb, kb, vb, li, lf = xs
        b = jnp.cumsum(lf, axis=-1)
        logD = b[..., :, None] - b[..., None, :] + li[..., None, :]
        logD = jnp.where(causal, logD, -jnp.inf)
        inter = b + m[..., None]
        m_t = jnp.maximum(inter, jnp.max(logD, axis=-1))
        Dm = jnp.exp(logD - m_t[..., None])
        sc = jnp.exp(inter - m_t)
        sk = jnp.einsum('bhtd,bhsd->bhts', qb, kb) * Dm
        num = jnp.einsum('bhts,bhsv->bhtv', sk, vb) + sc[..., None] * jnp.einsum('bhtd,bhdv->bhtv', qb, C)
        den = jnp.sum(sk, axis=-1) + sc * jnp.einsum('bhtd,bhd->bht', qb, n)
        hb = num / jnp.maximum(jnp.abs(den), jnp.exp(-m_t))[..., None]
        bL = b[..., -1]
        logw = bL[..., None] - b + li
        m_new = jnp.maximum(bL + m, jnp.max(logw, axis=-1))
        w = jnp.exp(logw - m_new[..., None])
        decay = jnp.exp(bL + m - m_new)
        C_new = decay[..., None, None] * C + jnp.einsum('bhs,bhsd,bhsv->bhdv', w, kb, vb)
        n_new = decay[..., None] * n + jnp.einsum('bhs,bhsd->bhd', w, kb)
        return (C_new, n_new, m_new), hb

    init = (jnp.zeros((bsz, h_, dqk, dv), jnp.float32),
            jnp.zeros((bsz, h_, dqk), jnp.float32),
            jnp.zeros((bsz, h_), jnp.float32))
    _, hs = lax.scan(step, init, (qc, kc, vc, lic, lfc))
    hs = hs.transpose(1, 0, 3, 2, 4).reshape(bsz, s, h_, dv)
    hs = hs * lax.rsqrt(jnp.mean(hs * hs, axis=-1, keepdims=True) + EPS) * head_norm.astype(jnp.float32)
    hs = hs.reshape(bsz, s, MLSTM_V) * o_gate
    return hs.astype(x.dtype) @ w_out


def short_conv_mixer(x, w_in, conv_w, conv_b, w_out):
    proj = x @ w_in
    bgate = proj[..., :SCONV_WIDTH]
    cgate = proj[..., SCONV_WIDTH:2 * SCONV_WIDTH]
    xin = proj[..., 2 * SCONV_WIDTH:]
    y = causal_dwconv(cgate * xin, conv_w, conv_b)
    return (bgate * y) @ w_out


def conv_ffn(x, w_up, conv_w, conv_b, w_down):
    u = causal_dwconv(x @ w_up, conv_w, conv_b)
    gate = u[..., :D_FF]
    val = u[..., D_FF:]
    return (jax.nn.silu(gate) * val) @ w_down


def setup_inputs(seed: int = 0) -> dict:
    key = jax.random.key(seed)
    ks = jax.random.split(key, 24)
    f32 = jnp.float32

    def nrm(k, shape, scale):
        return jax.random.normal(k, shape, f32) * scale

    def gain(k, shape):
        return 1.0 + 0.02 * jax.random.normal(k, shape, f32)

    b_i = 0.1 * jax.random.normal(ks[3], (MLSTM_HEADS,), f32)
    b_f = 3.0 + 0.5 * jax.random.normal(ks[4], (MLSTM_HEADS,), f32)
    return {
        "x": jax.random.normal(ks[0], (BATCH, SEQ, D_MODEL), f32),
        "l0_norm_mix": gain(ks[1], (D_MODEL,)),
        "l0_mlstm_w_in": nrm(ks[2], (D_MODEL, MLSTM_IN), D_MODEL ** -0.5),
        "l0_mlstm_b_gates": jnp.concatenate([b_i, b_f]),
        "l0_mlstm_head_norm": gain(ks[5], (MLSTM_HEADS, MLSTM_DV)),
        "l0_mlstm_w_out": nrm(ks[6], (MLSTM_V, D_MODEL), MLSTM_V ** -0.5),
        "l0_norm_ffn": gain(ks[7], (D_MODEL,)),
        "l0_ffn_w_up": nrm(ks[8], (D_MODEL, 2 * D_FF), D_MODEL ** -0.5),
        "l0_ffn_conv_w": nrm(ks[9], (CONV_K, 2 * D_FF), CONV_K ** -0.5),
        "l0_ffn_conv_b": nrm(ks[10], (2 * D_FF,), 0.02),
        "l0_ffn_w_down": nrm(ks[11], (D_FF, D_MODEL), D_FF ** -0.5),
        "l1_norm_mix": gain(ks[12], (D_MODEL,)),
        "l1_sconv_w_in": nrm(ks[13], (D_MODEL, 3 * SCONV_WIDTH), D_MODEL ** -0.5),
        "l1_sconv_conv_w": nrm(ks[14], (CONV_K, SCONV_WIDTH), CONV_K ** -0.5),
        "l1_sconv_conv_b": nrm(ks[15], (SCONV_WIDTH,), 0.02),
        "l1_sconv_w_out": nrm(ks[16], (SCONV_WIDTH, D_MODEL), SCONV_WIDTH ** -0.5),
        "l1_norm_ffn": gain(ks[17], (D_MODEL,)),
        "l1_ffn_w_up": nrm(ks[18], (D_MODEL, 2 * D_FF), D_MODEL ** -0.5),
        "l1_ffn_conv_w": nrm(ks[19], (CONV_K, 2 * D_FF), CONV_K ** -0.5),
        "l1_ffn_conv_b": nrm(ks[20], (2 * D_FF,), 0.02),
        "l1_ffn_w_down": nrm(ks[21], (D_FF, D_MODEL), D_FF ** -0.5),
        "final_norm": gain(ks[22], (D_MODEL,)),
    }


def reference(x, l0_norm_mix, l0_mlstm_w_in, l0_mlstm_b_gates, l0_mlstm_head_norm, l0_mlstm_w_out,
              l0_norm_ffn, l0_ffn_w_up, l0_ffn_conv_w, l0_ffn_conv_b, l0_ffn_w_down,
              l1_norm_mix, l1_sconv_w_in, l1_sconv_conv_w, l1_sconv_conv_b, l1_sconv_w_out,
              l1_norm_ffn, l1_ffn_w_up, l1_ffn_conv_w, l1_ffn_conv_b, l1_ffn_w_down,
              final_norm):
    mix_norms = [l0_norm_mix, l1_norm_mix]
    mix_params = [(l0_mlstm_w_in, l0_mlstm_b_gates, l0_mlstm_head_norm, l0_mlstm_w_out),
                  (l1_sconv_w_in, l1_sconv_conv_w, l1_sconv_conv_b, l1_sconv_w_out)]
    ffn_norms = [l0_norm_ffn, l1_norm_ffn]
    ffn_params = [(l0_ffn_w_up, l0_ffn_conv_w, l0_ffn_conv_b, l0_ffn_w_down),
                  (l1_ffn_w_up, l1_ffn_conv_w, l1_ffn_conv_b, l1_ffn_w_down)]
    mixers = [mlstm_mixer, short_conv_mixer]
    for i in range(DEPTH):
        mixer = mixers[i % N_MIXERS]
        x = x + mixer(rmsnorm(x, mix_norms[i]), *mix_params[i])
        x = x + conv_ffn(rmsnorm(x, ffn_norms[i]), *ffn_params[i])
    return rmsnorm(x, final_norm)
```

```python
import math
import numpy as np
from contextlib import ExitStack
import concourse.bass as bass
import concourse.mybir as mybir
from concourse.bass_utils import run_bass_kernel_spmd

F32 = mybir.dt.float32
BF16 = mybir.dt.bfloat16
AF = mybir.ActivationFunctionType
ALU = mybir.AluOpType

NCORES = 8
D = 1024
KC = 8
NTOK = 2048
HALO = 128
XW = HALO + NTOK
NPRE = 1920
DFF = 2816
NFB = 22
EPS = 1e-6
SLOT = 16448
FFN_GROUPS = [(0, 5), (5, 10), (10, 14), (14, 18), (18, 22)]
SC_GROUPS = [(0, 4), (4, 8)]
OUT_RANGES = [(124, 535), (535, 946), (946, 1357), (1357, 1768), (1768, 2176)]
XN0 = 120
XNW = XW - XN0

V_G = 0
V_FC = 40
V_SC = V_FC + 352
V_HN = V_SC + 32
V_BG = V_HN + 8
V_FLAG = V_BG + 8
NV = V_FLAG + 1


class Instr:
    __slots__ = ("eng", "fn", "deps", "signals", "sigval", "is_dma", "sem", "semval", "idx", "semkey", "lbl")


class Prog:
    ENGS = ("pe", "act", "dve", "pool", "sp")

    def __init__(self, nc, stack):
        self.nc = nc
        self.stack = stack
        self.lists = {e: [] for e in self.ENGS}
        self.last_w = {}
        self.readers = {}
        self.dma_sems = {}
        self.engsem = {}
        self.fence = []
        self.fenced = {e: True for e in self.ENGS}
        self.lbl = ""
        self.names = {}

    def _sem(self, name):
        return self.stack.enter_context(self.nc.semaphore(name))

    def set_fence(self):
        self.fence = []
        for e in self.ENGS:
            if self.lists[e]:
                self.fence.append(self.lists[e][-1])
            self.fenced[e] = False

    def add(self, eng, fn, reads=(), writes=(), dma_sem=None):
        ins = Instr()
        ins.eng = eng
        ins.fn = fn
        ins.signals = False
        ins.sigval = 0
        ins.is_dma = dma_sem is not None
        ins.idx = len(self.lists[eng])
        ins.sem = None
        ins.semval = 0
        ins.semkey = dma_sem
        ins.lbl = self.lbl
        deps = []
        seen = set()

        def dep(d, force=False):
            if d is None or id(d) in seen:
                return
            seen.add(id(d))
            if (not force) and (not d.is_dma) and d.eng == eng and not ins.is_dma:
                if eng == "pe":
                    return
                if ins.idx - d.idx > 3:
                    return
            if d.is_dma and ins.is_dma and d.semkey == ins.semkey:
                return
            deps.append(d)
            if not d.is_dma:
                d.signals = True

        for k in reads:
            dep(self.last_w.get(k))
        for k in writes:
            dep(self.last_w.get(k))
            rd = self.readers.get(k)
            if rd:
                for r in rd.values():
                    dep(r)
        if not self.fenced[eng] and eng in ("pe", "act", "dve"):
            self.fenced[eng] = True
            for f in self.fence:
                if f.eng != eng or f.is_dma:
                    dep(f, force=True)
        ins.deps = deps
        if ins.is_dma:
            if dma_sem not in self.dma_sems:
                self.dma_sems[dma_sem] = [self._sem("d_" + str(dma_sem)), 0]
            ent = self.dma_sems[dma_sem]
            ent[1] += 16
            ins.sem = ent[0]
            ins.semval = ent[1]
        for k in reads:
            rd = self.readers.setdefault(k, {})
            rd[("dma", id(ins)) if ins.is_dma else eng] = ins
        for k in writes:
            self.last_w[k] = ins
            self.readers[k] = {}
        self.lists[eng].append(ins)
        return ins

    def emit(self, final_waits=()):
        nc = self.nc
        for e in self.ENGS:
            self.engsem[e] = self._sem("e_" + e)
            cnt = 0
            for ins in self.lists[e]:
                if ins.signals and not ins.is_dma:
                    cnt += 1
                    ins.sigval = cnt
        engsem = self.engsem
        lists = self.lists

        def run(e, handle):
            waited = {}
            for ins in lists[e]:
                needs = {}
                for d in ins.deps:
                    if d.is_dma:
                        s, v = d.sem, d.semval
                    else:
                        s, v = engsem[d.eng], d.sigval
                    key = id(s)
                    if key not in needs or needs[key][1] < v:
                        needs[key] = (s, v)
                for key, (s, v) in needs.items():
                    if waited.get(key, 0) < v:
                        handle.wait_ge(s, v)
                        waited[key] = v
                inst = ins.fn(handle)
                try:
                    self.names[str(inst.ins.name)] = ins.lbl
                except Exception:
                    pass
                if ins.is_dma:
                    inst.then_inc(ins.sem, 16)
                elif ins.signals:
                    inst.then_inc(engsem[e], 1)
            if e == "sp":
                for k in final_waits:
                    ent = self.dma_sems[k]
                    handle.wait_ge(ent[0], ent[1])

        with nc.Block() as block:
            @block.tensor
            def _(h):
                run("pe", h)

            @block.scalar
            def _(h):
                run("act", h)

            @block.vector
            def _(h):
                run("dve", h)

            @block.gpsimd
            def _(h):
                run("pool", h)

            @block.sync
            def _(h):
                run("sp", h)


def xkeys(c0, c1):
    return ["X%d" % c for c in range(c0 // 128, (c1 - 1) // 128 + 1)]


def build_program(upto=5):
    nc = bass.Bass("TRN2", target_bir_lowering=False)
    dt_in = lambda n, s: nc.dram_tensor(n, s, F32, kind="ExternalInput").ap()
    xm = dt_in("xm", [D, XW])
    xp = dt_in("xp", [D, NPRE])
    vec_d = dt_in("vec", [128, NV])
    cf_d = dt_in("cf", [128, 256])
    cb_d = dt_in("cb", [128, 128 + 512])
    w0_in = dt_in("l0_w_in", [D, 3080])
    w0_out = dt_in("l0_w_out", [D, D])
    w_up = [dt_in("l0_w_up", [D, 2 * DFF]), dt_in("l1_w_up", [D, 2 * DFF])]
    w_dn = [dt_in("l0_w_down", [DFF, D]), dt_in("l1_w_down", [DFF, D])]
    w1_in = dt_in("l1_w_in", [D, 3 * D])
    w1_out = dt_in("l1_w_out", [D, D])
    y = nc.dram_tensor("y", [D, NTOK], F32, kind="ExternalOutput").ap()
    import os
    DBG = os.environ.get("K_DBG", "0") == "1"
    if DBG:
        dbg_d = nc.dram_tensor("dbg", [128, 2048], F32, kind="ExternalOutput").ap()

    kview = lambda w: w.rearrange("(kc p) n -> p kc n", p=128)

    with ExitStack() as st:
        P = Prog(nc, st)
        sb = lambda n, s, d: st.enter_context(nc.sbuf_tensor(n, s, d))
        X = sb("X", [128, KC, XW], F32)
        RA = sb("RA", [128, 2 * SLOT], BF16)
        CX = sb("CX", [128, KC * XNW], BF16)
        SCF = sb("SCF", [128, 3550], F32)
        SCB = sb("SCB", [128, 12600], BF16)
        vec = sb("vecs", [128, NV], F32)
        cf = sb("cfs", [128, 256], F32)
        cb = sb("cbs", [128, 640], BF16)
        ones_bf = sb("ones_bf", [128, 128], BF16)
        banks = [st.enter_context(nc.psum_tensor("B%d" % i, [128, 512], F32)) for i in range(8)]
        B5bf = banks[5][:].bitcast(BF16)

        tri = cf[:, 0:128]
        ones_f = cf[:, 128:256]
        ident = cb[:, 0:128]
        mask4 = cb[:, 128:640]

        def MM(out, lhsT, rhs, start, stop, reads, writes):
            P.add("pe", lambda e: e.matmul(out, lhsT=lhsT, rhs=rhs, start=start, stop=stop), reads, writes)

        def TR(out, in_, reads, writes):
            P.add("pe", lambda e: e.transpose(out, in_, ident), list(reads) + ["cb"], writes)

        def ACT(out, in_, func, reads, writes, scale=None, bias=None, accum_out=None):
            kw = {}
            if scale is not None:
                kw["scale"] = scale
            if bias is not None:
                kw["bias"] = bias
            if accum_out is not None:
                kw["accum_out"] = accum_out
            P.add("act", lambda e: e.activation(out=out, in_=in_, func=func, **kw), reads, writes)

        def TT(out, in0, in1, op, reads, writes):
            P.add("dve", lambda e: e.tensor_tensor(out=out, in0=in0, in1=in1, op=op), reads, writes)

        def TS(out, in0, s1, op0, reads, writes, s2=None, op1=None):
            if op1 is None:
                P.add("dve", lambda e: e.tensor_scalar(out=out, in0=in0, scalar1=s1, scalar2=None, op0=op0), reads, writes)
            else:
                P.add("dve", lambda e: e.tensor_scalar(out=out, in0=in0, scalar1=s1, scalar2=s2, op0=op0, op1=op1), reads, writes)

        def STT(out, in0, scalar, in1, op0, op1, reads, writes):
            P.add("dve", lambda e: e.scalar_tensor_tensor(out=out, in0=in0, scalar=scalar, in1=in1, op0=op0, op1=op1), reads, writes)

        def DMA(eng, out, in_, reads, writes, sem):
            P.add(eng, lambda e: e.dma_start(out=out, in_=in_), reads, writes, dma_sem=sem)

        class Bump:
            def __init__(self, t, n):
                self.t, self.n, self.o = t, n, 0

            def reset(self):
                self.o = 0

            def get(self, n):
                n2 = (n + 1) // 2 * 2
                assert self.o + n2 <= self.n, (self.o, n2, self.n)
                ap = self.t[:, self.o:self.o + n]
                self.o += n2
                return ap

        bF = Bump(SCF, 3550)
        bB = Bump(SCB, 12600)

        DMA("sp", vec[:], vec_d, [], ["vec"], "vec")
        DMA("sp", cf[:], cf_d, [], ["cf"], "cf")
        DMA("pool", cb[:], cb_d, [], ["cb"], "cb")
        P.add("dve", lambda e: e.memset(ones_bf[:], 1.0), [], ["ones_bf"])

        slot = [RA[:, 0:SLOT], RA[:, SLOT:2 * SLOT]]
        s0 = slot[0].rearrange("p (k n) -> p k n", k=KC)
        Wqkvg = s0
        Wo = slot[1][:, 0:8192].rearrange("p (k n) -> p k n", k=KC)
        Wout0 = slot[1][:, 8192:16384].rearrange("p (k n) -> p k n", k=KC)
        w0v = kview(w0_in)
        DMA("pool", Wqkvg[:, :, 2048:2056], w0v[:, :, 3072:3080], [], ["w_g"], "w_g")
        DMA("pool", Wqkvg[:, :, 512:1024], w0v[:, :, 512:1024], [], ["w_k"], "w_k")
        DMA("pool", Wqkvg[:, :, 1024:2048], w0v[:, :, 1024:2048], [], ["w_v"], "w_v")
        DMA("pool", Wqkvg[:, :, 0:512], w0v[:, :, 0:512], [], ["w_q"], "w_q")
        DMA("pool", Wo, w0v[:, :, 2048:3072], [], ["slot1a"], "slot1a")
        DMA("pool", Wout0, kview(w0_out), [], ["slot1b"], "slot1b")

        def norm(src_fn, n, gcol, dst_fn, rkeys, wkeys, ssbank, sskey, sqs, lnv, rstd, rstd_psum=False):
            for kc in range(KC):
                sq = sqs[kc % 2][:, 0:n]
                ACT(sq, src_fn(kc), AF.Square, rkeys, ["sq%d" % (kc % 2)])
                MM(ssbank[:, 0:n], ones_bf[:], sq, kc == 0, kc == KC - 1, ["sq%d" % (kc % 2), "ones_bf"], [sskey])
            ACT(lnv[:, 0:n], ssbank[:, 0:n], AF.Ln, [sskey], ["lnv"], scale=1.0 / D, bias=EPS)
            if rstd_psum:
                ACT(ssbank[:, 0:n], lnv[:, 0:n], AF.Exp, ["lnv"], [sskey], scale=-0.5)
                rs_ap, rs_key = ssbank[:, 0:n], sskey
            else:
                ACT(rstd[:, 0:n], lnv[:, 0:n], AF.Exp, ["lnv"], ["rstd"], scale=-0.5)
                rs_ap, rs_key = rstd[:, 0:n], "rstd"
            for kc in range(KC):
                STT(dst_fn(kc), src_fn(kc), vec[:, gcol + kc:gcol + kc + 1], rs_ap, ALU.mult, ALU.mult,
                    list(rkeys) + [rs_key, "vec"], wkeys)

        sqs = [bB.get(512), bB.get(512)]
        lnv = bF.get(512)
        rstd = bF.get(512)
        ogtmp = bF.get(512)
        ntmp = bF.get(512)
        C_f = bF.get(1024)
        n_f = bF.get(4)
        small = [dict(gl=bF.get(8), e4=bF.get(4), l4=bF.get(4), A=bF.get(16), E=bF.get(16)) for _ in range(2)]
        dsb = bF.get(4)
        dab = bF.get(4)
        rden = bF.get(4)
        ssr = bF.get(4)
        t1 = bF.get(4)
        t2 = bF.get(4)
        scl = bF.get(4)
        tok = [dict(qs=bB.get(512), ks=bB.get(512), kw=bB.get(512), v=bB.get(1024)) for _ in range(2)]
        qkT = [bB.get(1024), bB.get(1024)]
        SmT = [bB.get(512), bB.get(512)]
        hst = [bB.get(1024), bB.get(1024)]
        C_bf = bB.get(1024)
        n_bf = bB.get(4)
        junk = bB.get(256)
        xnb = [CX[:, 0:4096].rearrange("p (k n) -> p k n", k=KC), CX[:, 4096:8192].rearrange("p (k n) -> p k n", k=KC)]
        ogT = CX[:, 8192:12288].rearrange("p (k n) -> p k n", k=KC)
        hsT = CX[:, 12288:16384].rearrange("p (k n) -> p k n", k=KC)

        P.add("dve", lambda e: e.memset(C_f, 0.0), [], ["Cf"])
        P.add("dve", lambda e: e.memset(n_f, 0.0), [], ["nf"])
        P.add("dve", lambda e: e.memset(C_bf, 0.0), [], ["Cbf"])
        P.add("dve", lambda e: e.memset(n_bf, 0.0), [], ["nbf"])

        groups = []
        for j, (t0, nt) in enumerate([(0, 512), (512, 512), (1024, 512), (1536, 384)]):
            groups.append(dict(xc=128 + 512 * (j % 2), n=nt, full=False, src=xp, sc=t0))
        groups.append(dict(xc=0, n=128, full=True, src=xm, sc=0))
        for k in range(4):
            groups.append(dict(xc=128 + 512 * k, n=512, full=True, src=xm, sc=128 + 512 * k))
        xmv = kview(xm)
        xpv = kview(xp)

        def load_group(gi):
            g = groups[gi]
            v = xpv if g["src"] is xp else xmv
            ks = xkeys(g["xc"], g["xc"] + g["n"])
            DMA("sp", X[:, :, g["xc"]:g["xc"] + g["n"]], v[:, :, g["sc"]:g["sc"] + g["n"]], [], ks, "xld%d" % gi)

        for gi in (0, 1, 4, 7, 8):
            load_group(gi)
        LOAD_AFTER_NORM = {0: 2, 1: 3, 2: 5, 3: 6}

        chunks = []
        for gi, g in enumerate(groups):
            nch = g["n"] // 128
            for c in range(nch):
                chunks.append(dict(g=gi, c=c, first=(c == 0), last=(c == nch - 1), full=g["full"]))

        LNSCALE = math.log(128.0 ** -0.5)

        def do_norm(gi):
            P.lbl = "do_norm(%d)" % gi
            g = groups[gi]
            xc, n = g["xc"], g["n"]
            xn = xnb[gi % 2]
            norm(lambda kc: X[:, kc, xc:xc + n], n, V_G + 0, lambda kc: xn[:, kc, 0:n],
                 xkeys(xc, xc + n), ["xn%d" % (gi % 2)], banks[6], "B6", sqs, lnv, rstd)

        def do_ogate(gi):
            P.lbl = "do_ogate(%d)" % gi
            g = groups[gi]
            n = g["n"]
            xn = xnb[gi % 2]
            for vb in range(8):
                bk = (6, 7, 3, 4)[vb % 4]
                for kc in range(KC):
                    MM(banks[bk][:, 0:n], Wo[:, kc, vb * 128:(vb + 1) * 128], xn[:, kc, 0:n], kc == 0, kc == KC - 1,
                       ["slot1a", "xn%d" % (gi % 2)], ["B%d" % bk])
                if USE_SIGMOID:
                    ACT(ogtmp[:, 0:n], banks[bk][:, 0:n], AF.Sigmoid, ["B%d" % bk], ["ogtmp"])
                else:
                    ACT(ogtmp[:, 0:n], banks[bk][:, 0:n], AF.Exp, ["B%d" % bk], ["ogtmp"], scale=-1.0)
                    ACT(ogtmp[:, 0:n], ogtmp[:, 0:n], AF.Ln, ["ogtmp"], ["ogtmp"], bias=1.0)
                    ACT(ogtmp[:, 0:n], ogtmp[:, 0:n], AF.Exp, ["ogtmp"], ["ogtmp"], scale=-1.0)
                TS(ogT[:, vb, 0:n], ogtmp[:, 0:n], vec[:, V_HN + vb:V_HN + vb + 1], ALU.mult, ["ogtmp", "vec"], ["ogT"])

        def do_wout(gi):
            P.lbl = "do_wout(%d)" % gi
            g = groups[gi]
            xc, n = g["xc"], g["n"]
            for dc in range(8):
                bk = (6, 7, 3, 4)[dc % 4]
                for kc in range(KC):
                    MM(banks[bk][:, 0:n], Wout0[:, kc, dc * 128:(dc + 1) * 128], hsT[:, kc, 0:n], kc == 0, kc == KC - 1,
                       ["slot1b", "hsT"], ["B%d" % bk])
                TT(X[:, dc, xc:xc + n], X[:, dc, xc:xc + n], banks[bk][:, 0:n], ALU.add,
                   ["B%d" % bk] + xkeys(xc, xc + n), xkeys(xc, xc + n))

        def A_gates(i):
            P.lbl = "A_gates(%d)" % i
            ch = chunks[i]
            par = i % 2
            xn = xnb[ch["g"] % 2]
            xk = "xn%d" % (ch["g"] % 2)
            cs = ch["c"] * 128
            sm = small[par]
            for kc in range(KC):
                MM(banks[0][:, 0:8], xn[:, kc, cs:cs + 128], Wqkvg[:, kc, 2048:2056], kc == 0, kc == KC - 1,
                   [xk, "w_g"], ["B0"])
            TT(sm["gl"], banks[0][:, 0:8], vec[:, V_BG:V_BG + 8], ALU.add, ["B0", "vec"], ["gl%d" % par])
            ACT(sm["e4"], sm["gl"][:, 4:8], AF.Exp, ["gl%d" % par], ["e4%d" % par], scale=-1.0)
            ACT(sm["l4"], sm["e4"], AF.Ln, ["e4%d" % par], ["l4%d" % par], bias=1.0)

        def A_cs(i):
            P.lbl = "A_cs(%d)" % i
            par = i % 2
            sm = small[par]
            MM(banks[0][:, 8:12], tri, sm["l4"], True, True, ["cf", "l4%d" % par], ["B0"])
            MM(banks[0][:, 12:16], ones_f, sm["l4"], True, True, ["cf", "l4%d" % par], ["B0"])
            A = sm["A"]
            TT(A[:, 4:8], sm["gl"][:, 0:4], banks[0][:, 8:12], ALU.add, ["gl%d" % par, "B0"], ["A%d" % par])
            TT(A[:, 8:12], A[:, 4:8], banks[0][:, 12:16], ALU.subtract, ["A%d" % par, "B0"], ["A%d" % par])
            TS(A[:, 0:4], banks[0][:, 8:12], -1.0, ALU.mult, ["B0"], ["A%d" % par], s2=LNSCALE, op1=ALU.add)
            TS(A[:, 12:16], banks[0][:, 12:16], -1.0, ALU.mult, ["B0"], ["A%d" % par])
            ACT(sm["E"], A, AF.Exp, ["A%d" % par], ["E%d" % par])

        def bc4(ap4, w):
            return ap4.unsqueeze(2).to_broadcast([128, 4, w])

        def v3(ap, k):
            return ap.rearrange("p (h d) -> p h d", h=k)

        def A_qk(i, which="qk"):
            P.lbl = "A_qk(%d)" % i
            ch = chunks[i]
            par = i % 2
            xn = xnb[ch["g"] % 2]
            xk = "xn%d" % (ch["g"] % 2)
            cs = ch["c"] * 128
            if ch["full"] and which in ("q", "qk"):
                for kc in range(KC):
                    MM(banks[1][:], xn[:, kc, cs:cs + 128], Wqkvg[:, kc, 0:512], kc == 0, kc == KC - 1, [xk, "w_q"], ["B1"])
            if which in ("k", "qk"):
                for kc in range(KC):
                    MM(banks[2][:], xn[:, kc, cs:cs + 128], Wqkvg[:, kc, 512:1024], kc == 0, kc == KC - 1, [xk, "w_k"], ["B2"])

        def A_qk_evac(i):
            P.lbl = "A_qk_evac(%d)" % i
            ch = chunks[i]
            par = i % 2
            E = small[par]["E"]
            tk = tok[par]
            if ch["full"]:
                TT(v3(tk["qs"], 4), v3(banks[1][:], 4), bc4(E[:, 0:4], 128), ALU.mult, ["B1", "E%d" % par], ["qs%d" % par])
                TT(v3(tk["ks"], 4), v3(banks[2][:], 4), bc4(E[:, 4:8], 128), ALU.mult, ["B2", "E%d" % par], ["ks%d" % par])
            TT(v3(tk["kw"], 4), v3(banks[2][:], 4), bc4(E[:, 8:12], 128), ALU.mult, ["B2", "E%d" % par], ["kw%d" % par])

        normed = set()
        pending_stt = []
        USE_SIGMOID = os.environ.get("K_SIGMOID", "1") == "1"
        USE_POOL_SQ = os.environ.get("K_POOLSQ", "1") == "1"

        def flush_stt(k):
            while pending_stt and k > 0:
                pending_stt.pop(0)[1]()
                k -= 1

        def A_v(i, norm_gi=None, halves=(0, 1)):
            P.lbl = "A_v(%d)" % i
            ch = chunks[i]
            par = i % 2
            xn = xnb[ch["g"] % 2]
            xk = "xn%d" % (ch["g"] % 2)
            cs = ch["c"] * 128
            tk = tok[par]
            pairs = []
            pairs_made = False
            if norm_gi is not None and norm_gi not in normed:
                pairs_made = True
                flush_stt(len(pending_stt))
                normed.add(norm_gi)
                g2 = groups[norm_gi]
                xc2, n2 = g2["xc"], g2["n"]
                xn2 = xnb[norm_gi % 2]
                rk = xkeys(xc2, xc2 + n2)

                def mkpair(kc):
                    def f():
                        lb = P.lbl
                        P.lbl = "normA(%d)" % norm_gi
                        sq = sqs[kc % 2][:, 0:n2]
                        xin = X[:, kc, xc2:xc2 + n2]
                        if USE_POOL_SQ:
                            P.add("pool", lambda e: e.tensor_tensor(out=sq, in0=xin, in1=xin, op=ALU.mult), rk, ["sq%d" % (kc % 2)])
                        else:
                            ACT(sq, xin, AF.Square, rk, ["sq%d" % (kc % 2)])
                        MM(banks[5][:, 0:n2], ones_bf[:], sq, kc == 0, kc == KC - 1, ["sq%d" % (kc % 2), "ones_bf"], ["B5"])
                        P.lbl = lb
                    return f
                pairs = [mkpair(kc) for kc in range(KC)]

                def fin():
                    P.lbl = "normA(%d)" % norm_gi
                    ACT(lnv[:, 0:n2], banks[5][:, 0:n2], AF.Ln, ["B5"], ["lnv"], scale=1.0 / D, bias=EPS)
                    ACT(rstd[:, 0:n2], lnv[:, 0:n2], AF.Exp, ["lnv"], ["rstd"], scale=-0.5)

                def mkstt(kc):
                    def f():
                        P.lbl = "normB(%d)" % norm_gi
                        STT(xn2[:, kc, 0:n2], X[:, kc, xc2:xc2 + n2], vec[:, V_G + kc:V_G + kc + 1], rstd[:, 0:n2],
                            ALU.mult, ALU.mult, rk + ["rstd", "vec"], ["xn%d" % (norm_gi % 2)])
                        if kc == KC - 1 and norm_gi in LOAD_AFTER_NORM:
                            load_group(LOAD_AFTER_NORM[norm_gi])
                    return f
                for kc in range(KC):
                    pending_stt.append((norm_gi, mkstt(kc)))
            for hh in halves:
                for kc in range(KC):
                    MM(banks[3 + hh][:], xn[:, kc, cs:cs + 128], Wqkvg[:, kc, 1024 + 512 * hh:1536 + 512 * hh],
                       kc == 0, kc == KC - 1, [xk, "w_v"], ["B%d" % (3 + hh)])
                    if pairs and (len(halves) == 1 or kc % 2 == 1):
                        pairs.pop(0)()
                ACT(tk["v"][:, 512 * hh:512 * hh + 512], banks[3 + hh][:], AF.Copy, ["B%d" % (3 + hh)], ["v%d_%d" % (par, hh)])
            if pairs_made:
                while pairs:
                    pairs.pop(0)()
                fin()

        def B_T(i):
            P.lbl = "B_T(%d)" % i
            par = i % 2
            tk = tok[par]
            for h in range(4):
                TR(B5bf[:, h * 128:(h + 1) * 128], tk["qs"][:, h * 128:(h + 1) * 128], ["qs%d" % par], ["B5"])
            for h in range(4):
                TR(B5bf[:, 512 + h * 128:512 + (h + 1) * 128], tk["ks"][:, h * 128:(h + 1) * 128], ["ks%d" % par], ["B5"])
            ACT(qkT[par], B5bf[:, 0:1024], AF.Copy, ["B5"], ["qkT%d" % par])

        def B_S(i):
            P.lbl = "B_S(%d)" % i
            par = i % 2
            for h in range(4):
                MM(banks[5][:, h * 128:(h + 1) * 128], qkT[par][:, 512 + h * 128:512 + (h + 1) * 128],
                   qkT[par][:, h * 128:(h + 1) * 128], True, True, ["qkT%d" % par], ["B5"])
            TT(SmT[par], banks[5][:], mask4, ALU.mult, ["B5", "cb"], ["SmT%d" % par])

        def B_num(i):
            P.lbl = "B_num(%d)" % i
            par = i % 2
            tk = tok[par]
            for h in range(4):
                bk = 6 + h // 2
                o = banks[bk][:, (h % 2) * 256:(h % 2) * 256 + 256]
                MM(o, SmT[par][:, h * 128:(h + 1) * 128], tk["v"][:, h * 256:(h + 1) * 256], True, False,
                   ["SmT%d" % par, "v%d_%d" % (par, h // 2)], ["B%d" % bk])
                MM(o, qkT[par][:, h * 128:(h + 1) * 128], C_bf[:, h * 256:(h + 1) * 256], False, True,
                   ["qkT%d" % par, "Cbf"], ["B%d" % bk])
            for h in range(4):
                o = banks[0][:, 16 + h:17 + h]
                MM(o, SmT[par][:, h * 128:(h + 1) * 128], ones_bf[:, 0:1], True, False, ["SmT%d" % par, "ones_bf"], ["B0"])
                MM(o, qkT[par][:, h * 128:(h + 1) * 128], n_bf[:, h:h + 1], False, True, ["qkT%d" % par, "nbf"], ["B0"])

        def B_hs(i):
            P.lbl = "B_hs(%d)" % i
            par = i % 2
            for h in range(4):
                bk = 6 + h // 2
                ACT(junk, banks[bk][:, (h % 2) * 256:(h % 2) * 256 + 256], AF.Square, ["B%d" % bk], ["junk", "ssr"],
                    accum_out=ssr[:, h:h + 1])
            ACT(dsb, banks[0][:, 16:20], AF.Copy, ["B0"], ["dsb"])
            STT(dab, dsb, -1.0, dsb, ALU.mult, ALU.max, ["dsb"], ["dab"])
            TS(dab, dab, 1.0, ALU.max, ["dab"], ["dab"])
            TT(t1, dab, dab, ALU.mult, ["dab"], ["t1"])
            STT(t2, t1, 256.0 * EPS, ssr, ALU.mult, ALU.add, ["t1", "ssr"], ["t2"])
            ACT(t1, t2, AF.Ln, ["t2"], ["t1"], scale=1.0 / 256.0)
            ACT(scl, t1, AF.Exp, ["t1"], ["scl"], scale=-0.5)
            for h in range(4):
                bk = 6 + h // 2
                src = banks[bk][:, (h % 2) * 256:(h % 2) * 256 + 256]
                dst = hst[par][:, h * 256:(h + 1) * 256]
                ACT(dst, src, AF.Identity, ["B%d" % bk, "scl"], ["hst%d_%d" % (par, h)], scale=scl[:, h:h + 1])

        def B_dC(i):
            P.lbl = "B_dC(%d)" % i
            par = i % 2
            tk = tok[par]
            E = small[par]["E"]
            bb0 = 1 if chunks[i]["full"] else 6
            for h in range(4):
                bk = bb0 + h // 2
                MM(banks[bk][:, (h % 2) * 256:(h % 2) * 256 + 256], tk["kw"][:, h * 128:(h + 1) * 128],
                   tk["v"][:, h * 256:(h + 1) * 256], True, True, ["kw%d" % par, "v%d_%d" % (par, h // 2)], ["B%d" % bk])
            for h in range(4):
                MM(banks[0][:, 20 + h:21 + h], tk["kw"][:, h * 128:(h + 1) * 128], ones_bf[:, 0:1], True, True,
                   ["kw%d" % par, "ones_bf"], ["B0"])
            TT(n_f, n_f, E[:, 12:16], ALU.mult, ["nf", "E%d" % par], ["nf"])
            TT(n_f, n_f, banks[0][:, 20:24], ALU.add, ["nf", "B0"], ["nf"])
            for h in range(4):
                bk = bb0 + h // 2
                STT(C_f[:, h * 256:(h + 1) * 256], C_f[:, h * 256:(h + 1) * 256], E[:, 12 + h:13 + h],
                    banks[bk][:, (h % 2) * 256:(h % 2) * 256 + 256], ALU.mult, ALU.add,
                    ["Cf", "E%d" % par, "B%d" % bk], ["Cf"])
            ACT(C_bf, C_f, AF.Copy, ["Cf"], ["Cbf"])
            ACT(n_bf, n_f, AF.Copy, ["nf"], ["nbf"])

        B3bf = banks[3][:].bitcast(BF16)

        def B_hT(i, late=False):
            P.lbl = "B_hT(%d)" % i
            ch = chunks[i]
            par = i % 2
            cs = ch["c"] * 128
            pb_, pk_ = (B3bf, "B3") if late else (B5bf, "B5")
            for vb in range(8):
                TR(pb_[:, vb * 128:(vb + 1) * 128], hst[par][:, vb * 128:(vb + 1) * 128], ["hst%d_%d" % (par, vb // 2)], [pk_])
            TT(hsT[:, :, cs:cs + 128], v3(pb_[:, 0:1024], 8), ogT[:, :, cs:cs + 128], ALU.mult, [pk_, "ogT"], ["hsT"])

        def ensure_norm(gi):
            if gi not in normed:
                normed.add(gi)
                do_norm(gi)
                if gi in LOAD_AFTER_NORM:
                    load_group(LOAD_AFTER_NORM[gi])

        def A_all_first(i):
            ensure_norm(chunks[i]["g"])
            A_gates(i)
            A_cs(i)
            A_qk(i)
            A_qk_evac(i)
            A_v(i)

        A_all_first(0)
        nchunks = len(chunks)
        carrier = {1: 1}
        late_hT = [None]
        LATE_HT = os.environ.get("K_LATEHT", "1") == "1"
        for G in range(2, len(groups)):
            cs_ = [ci for ci, c in enumerate(chunks) if c["g"] == G - 2]
            carrier[max(max(cs_), min(cs_) + 1)] = G
        import os
        FENCE_IT = os.environ.get("K_FENCE", "0") == "1"
        for i in range(nchunks):
            ch = chunks[i]
            nx = i + 1 if i + 1 < nchunks else None
            if FENCE_IT:
                P.set_fence()
            if i >= int(os.environ.get("K_STOPIT", "999")):
                break
            ngi = None
            if nx is not None:
                ensure_norm(chunks[nx]["g"])
                ngi = carrier.get(nx)
                A_gates(nx)
                A_qk(nx, "q")
            if late_hT[0] is not None:
                B_hT(late_hT[0], late=True)
                late_hT[0] = None
            if ch["full"]:
                B_T(i)
                if ch["first"]:
                    do_ogate(ch["g"])
            if nx is not None:
                A_qk(nx, "k")
            if ch["full"]:
                B_S(i)
            if nx is not None:
                A_cs(nx)
                A_qk_evac(nx)
            flush_stt(2)
            if ch["full"]:
                if nx is not None:
                    A_v(nx, None, halves=(0,))
                B_num(i)
                B_hs(i)
                if nx is not None:
                    A_v(nx, ngi, halves=(1,))
                B_dC(i)
            else:
                B_dC(i)
                if nx is not None:
                    A_v(nx, ngi)
            need_all = (i + 2 < nchunks and chunks[i + 2]["first"] and any(g_ == chunks[i + 2]["g"] for g_, _ in pending_stt))
            flush_stt(len(pending_stt) if need_all else 2)
            if ch["full"]:
                if ch["last"] or nx is None or not LATE_HT:
                    B_hT(i)
                    if ch["last"]:
                        do_wout(ch["g"])
                else:
                    late_hT[0] = i

        flush_stt(len(pending_stt))
        if DBG and upto == 1:
            DMA("sp", dbg_d[:, 0:1024].rearrange("p (k n) -> p k n", k=KC), X[:, :, 0:128], ["X0"], [], "dbgout")
            DMA("sp", dbg_d[:, 1024:2048], C_f, ["Cf"], [], "dbgout")

        def dump_x():
            yv = kview(y)
            for k in range(4):
                c0 = 128 + 512 * k
                DMA("sp", yv[:, :, 512 * k:512 * k + 512], X[:, :, c0:c0 + 512], xkeys(c0, c0 + 512), [], "yout")

        XN = CX[:, 0:KC * XNW].rearrange("p (k n) -> p k n", k=KC)
        NORM_RANGES = [(120, 535)] + OUT_RANGES[1:]
        TW = 416

        def proj_phase(kind, layer, gcol, first_xn):
            P.set_fence()
            bF.reset()
            bB.reset()
            sqs2 = [bB.get(512), bB.get(512)]
            lnv2 = bF.get(512)
            rstd2 = bF.get(512)
            if kind == "ffn":
                ag = [bF.get(TW), bF.get(TW)]
                av = [bF.get(TW), bF.get(TW)]
                sg = [bB.get(TW), bB.get(TW)]
                tp1 = [bF.get(TW), bF.get(TW)]
                NBMAX = 5
                grp = FFN_GROUPS
            else:
                tc_ = [bF.get(TW), bF.get(TW)]
                cx = [bF.get(TW), bF.get(TW)]
                yy = [bF.get(TW), bF.get(TW)]
                NBMAX = 4
                grp = SC_GROUPS
            H = [bB.get(NBMAX * TW).rearrange("p (j n) -> p j n", j=NBMAX) for _ in range(2)]
            for k, (a, b) in enumerate(NORM_RANGES):
                wk = ["xn_t%d" % k] + (["xn0", "xn1", "ogT", "hsT"] if first_xn else [])
                norm(lambda kc, a=a, b=b: X[:, kc, a:b], b - a, gcol,
                     lambda kc, a=a, b=b: XN[:, kc, a - XN0:b - XN0], xkeys(a, b), wk, banks[6 + k % 2], "B%d" % (6 + k % 2),
                     sqs2, lnv2, rstd2, rstd_psum=RSTD_PSUM)
            gstate = {}

            def load_w(gidx):
                b0, b1 = grp[gidx]
                nb = b1 - b0
                sl = gstate["slotbase"] + gidx
                s = slot[sl % 2]
                sk = ["slot0", "w_g", "w_k", "w_v", "w_q"] if sl % 2 == 0 else ["slot1a", "slot1b"]
                sem = "slot%d" % (sl % 2)
                if kind == "ffn":
                    wu = kview(w_up[layer])
                    g_ap = s[:, 0:KC * nb * 128].rearrange("p (k n) -> p k n", k=KC)
                    v_ap = s[:, KC * nb * 128:2 * KC * nb * 128].rearrange("p (k n) -> p k n", k=KC)
                    d_ap = s[:, 2 * KC * nb * 128:2 * KC * nb * 128 + nb * 1024].rearrange("p (j n) -> p j n", j=nb)
                    DMA("pool", g_ap, wu[:, :, b0 * 128:b1 * 128], [], sk, sem)
                    DMA("pool", v_ap, wu[:, :, DFF + b0 * 128:DFF + b1 * 128], [], sk, sem)
                    DMA("pool", d_ap, w_dn[layer][b0 * 128:b1 * 128, :].rearrange("(j p) n -> p j n", p=128), [], sk, sem)
                    return dict(parts=[g_ap, v_ap], down=d_ap, keys=sk, nb=nb, b0=b0)
                else:
                    wi = kview(w1_in)
                    aps = []
                    for part in range(3):
                        ap = s[:, part * KC * nb * 128:(part + 1) * KC * nb * 128].rearrange("p (k n) -> p k n", k=KC)
                        DMA("pool", ap, wi[:, :, part * D + b0 * 128:part * D + b1 * 128], [], sk, sem)
                        aps.append(ap)
                    d_ap = s[:, 3 * KC * nb * 128:3 * KC * nb * 128 + nb * 1024].rearrange("p (j n) -> p j n", j=nb)
                    DMA("pool", d_ap, w1_out[b0 * 128:b1 * 128, :].rearrange("(j p) n -> p j n", p=128), [], sk, sem)
                    return dict(parts=aps, down=d_ap, keys=sk, nb=nb, b0=b0)

            return dict(kind=kind, layer=layer, grp=grp, H=H, load_w=load_w, gstate=gstate, tp1=(tp1 if kind == "ffn" else None),
                        bufs=(ag, av, sg) if kind == "ffn" else (tc_, cx, yy))

        slot_counter = [0]
        USE_POOL_SCTAP = os.environ.get("K_POOLSCTAP", "0") == "1"
        RSTD_PSUM = os.environ.get("K_RSTDPSUM", "0") == "1"
        USE_POOL_H = os.environ.get("K_POOLH", "1") == "1"
        USE_ACT_TAP = os.environ.get("K_ACTTAP", "0") == "1"

        def run_proj(ph, preloaded=None):
            kind, layer, grp, H = ph["kind"], ph["layer"], ph["grp"], ph["H"]
            ph["gstate"]["slotbase"] = slot_counter[0]
            ng = len(grp)
            W = {}
            W[0] = ph["load_w"](0)
            if ng > 1:
                W[1] = ph["load_w"](1)
            cnt = [0]
            nparts = 2 if kind == "ffn" else 3
            upb = [0, 1, 2, 3] if kind == "ffn" else [0, 1, 2, 3, 4, 5]
            dnb = [4, 5, 6, 7] if kind == "ffn" else [5, 6, 7]
            dcnt = [0]

            def up(gidx, k, mid=None):
                P.lbl = "up(%s%d,%d,%d)" % (kind, layer, gidx, k)
                w = W[gidx]
                o0, o1 = OUT_RANGES[k]
                p0, n = o0 - 2, o1 - o0 + 2
                no = n - 2
                hb = H[(gidx * 5 + k) % 2]
                hkey = "H%d" % ((gidx * 5 + k) % 2)
                xkeysr = ["xn_t%d" % k] + (["xn_t%d" % (k - 1)] if k > 0 else [])
                for j in range(w["nb"]):
                    if mid is not None and j == (w["nb"] + 1) // 2:
                        lb_ = P.lbl
                        mid()
                        P.lbl = lb_
                    par = cnt[0] % 2
                    cnt[0] += 1
                    if kind == "ffn":
                        bks = [upb[par * nparts + q] for q in range(nparts)]
                        qorder = range(nparts)
                    else:
                        bks = [4, par, 2 + par]
                        qorder = (1, 2, 0)
                    for q in qorder:
                        for kc in range(KC):
                            MM(banks[bks[q]][:, 0:n], w["parts"][q][:, kc, j * 128:(j + 1) * 128],
                               XN[:, kc, p0 - XN0:p0 - XN0 + n], kc == 0, kc == KC - 1,
                               w["keys"] + xkeysr, ["B%d" % bks[q]])
                    blk = w["b0"] + j
                    if kind == "ffn":
                        ag, av, sg = ph["bufs"]
                        tp1 = ph["tp1"]
                        base = V_FC + layer * 176
                        outs = (ag[par], av[par])
                        for q in range(2):
                            cb_ = blk + q * NFB
                            wc = lambda t, cb_=cb_: vec[:, base + cb_ * 3 + t:base + cb_ * 3 + t + 1]
                            bc = vec[:, base + 132 + cb_:base + 132 + cb_ + 1]
                            pb = banks[bks[q]]
                            nm = ("ag%d" if q == 0 else "av%d") % par
                            ACT(outs[q][:, 0:no], pb[:, 2:n], AF.Identity, ["B%d" % bks[q], "vec"], [nm], scale=wc(2), bias=bc)
                            if USE_ACT_TAP:
                                t1_ = tp1[q][:, 0:no]
                                o_ = outs[q][:, 0:no]
                                ACT(t1_, pb[:, 1:n - 1], AF.Identity, ["B%d" % bks[q], "vec"], ["tp1_%d" % q], scale=wc(1))
                                P.add("pool", lambda e, o_=o_, t1_=t1_: e.tensor_tensor(out=o_, in0=o_, in1=t1_, op=ALU.add),
                                      [nm, "tp1_%d" % q], [nm])
                            else:
                                STT(outs[q][:, 0:no], pb[:, 1:n - 1], wc(1), outs[q][:, 0:no], ALU.mult, ALU.add,
                                    ["B%d" % bks[q], "vec", nm], [nm])
                            STT(outs[q][:, 0:no], pb[:, 0:n - 2], wc(0), outs[q][:, 0:no], ALU.mult, ALU.add,
                                ["B%d" % bks[q], "vec", nm], [nm])
                        ACT(sg[par][:, 0:no], ag[par][:, 0:no], AF.Silu, ["ag%d" % par], ["sg%d" % par])
                        if USE_POOL_H:
                            o_, a_, b_ = hb[:, j, 0:no], sg[par][:, 0:no], av[par][:, 0:no]
                            P.add("pool", lambda e, o_=o_, a_=a_, b_=b_: e.tensor_tensor(out=o_, in0=a_, in1=b_, op=ALU.mult),
                                  ["sg%d" % par, "av%d" % par], [hkey])
                        else:
                            TT(hb[:, j, 0:no], sg[par][:, 0:no], av[par][:, 0:no], ALU.mult, ["sg%d" % par, "av%d" % par], [hkey])
                    else:
                        tc_, cx, yy = ph["bufs"]
                        base = V_SC
                        wc = lambda t: vec[:, base + blk * 3 + t:base + blk * 3 + t + 1]
                        bc = vec[:, base + 24 + blk:base + 24 + blk + 1]
                        ACT(tc_[par][:, 0:n], banks[bks[1]][:, 0:n], AF.Copy, ["B%d" % bks[1]], ["tc%d" % par])
                        TT(cx[par][:, 0:n], tc_[par][:, 0:n], banks[bks[2]][:, 0:n], ALU.mult, ["tc%d" % par, "B%d" % bks[2]], ["cx%d" % par])
                        ACT(yy[par][:, 0:no], cx[par][:, 2:n], AF.Identity, ["cx%d" % par, "vec"], ["yy%d" % par], scale=wc(2), bias=bc)
                        if USE_POOL_SCTAP:
                            for t_, sh in ((1, 1), (0, 0)):
                                src_ = cx[par][:, sh:sh + no]
                                tmp_ = tc_[par][:, 0:no]
                                y_ = yy[par][:, 0:no]
                                w_ = wc(t_)
                                P.add("pool", lambda e, src_=src_, tmp_=tmp_, w_=w_: e.tensor_scalar(
                                    out=tmp_, in0=src_, scalar1=w_, scalar2=1.0, op0=ALU.mult, op1=ALU.mult),
                                    ["cx%d" % par, "vec"], ["tc%d" % par])
                                P.add("pool", lambda e, y_=y_, tmp_=tmp_: e.tensor_tensor(out=y_, in0=y_, in1=tmp_, op=ALU.add),
                                      ["yy%d" % par, "tc%d" % par], ["yy%d" % par])
                        else:
                            STT(yy[par][:, 0:no], cx[par][:, 1:n - 1], wc(1), yy[par][:, 0:no], ALU.mult, ALU.add,
                                ["cx%d" % par, "vec", "yy%d" % par], ["yy%d" % par])
                            STT(yy[par][:, 0:no], cx[par][:, 0:n - 2], wc(0), yy[par][:, 0:no], ALU.mult, ALU.add,
                                ["cx%d" % par, "vec", "yy%d" % par], ["yy%d" % par])
                        TT(hb[:, j, 0:no], yy[par][:, 0:no], banks[bks[0]][:, 2:n], ALU.mult, ["yy%d" % par, "B%d" % bks[0]], [hkey])

            def down(gidx, k, dcs=range(8)):
                P.lbl = "down(%s%d,%d,%d)" % (kind, layer, gidx, k)
                w = W[gidx]
                o0, o1 = OUT_RANGES[k]
                no = o1 - o0
                hb = H[(gidx * 5 + k) % 2]
                hkey = "H%d" % ((gidx * 5 + k) % 2)
                for dc in dcs:
                    bk = dnb[dcnt[0] % len(dnb)]
                    dcnt[0] += 1
                    for j in range(w["nb"]):
                        MM(banks[bk][:, 0:no], w["down"][:, j, dc * 128:(dc + 1) * 128], hb[:, j, 0:no],
                           j == 0, j == w["nb"] - 1, w["keys"] + [hkey], ["B%d" % bk])
                    TT(X[:, dc, o0:o1], X[:, dc, o0:o1], banks[bk][:, 0:no], ALU.add,
                       ["B%d" % bk] + xkeys(o0, o1), xkeys(o0, o1))

            seq = [(g, k) for g in range(ng) for k in range(5)]
            SPLIT_DOWN = os.environ.get("K_SPLITDOWN", "1") == "1"
            for idx, (g, k) in enumerate(seq):
                if idx > 0 and SPLIT_DOWN:
                    pg, pk_ = seq[idx - 1]
                    up(g, k, mid=lambda pg=pg, pk_=pk_: down(pg, pk_, range(0, 4)))
                    down(pg, pk_, range(4, 8))
                else:
                    up(g, k)
                    if idx > 0:
                        down(*seq[idx - 1])
                if k == 0 and g + 1 < ng and (g + 1) not in W:
                    W[g + 1] = ph["load_w"](g + 1)
            down(*seq[-1])
            slot_counter[0] += ng

        final_keys = ["yout"] + (["dbgout"] if (DBG and upto == 1) else [])
        if upto >= 2:
            ph = proj_phase("ffn", 0, V_G + 8, True)
            run_proj(ph)
            TS(X[:, :, 120:128], X[:, :, 120:128], vec[:, V_FLAG:V_FLAG + 1], ALU.mult, ["X0", "vec"], ["X0"])
        if upto >= 3:
            ph = proj_phase("sconv", 1, V_G + 16, False)
            run_proj(ph)
        if upto >= 4:
            ph = proj_phase("ffn", 1, V_G + 24, False)
            run_proj(ph)
        if upto >= 5:
            P.set_fence()
            bF.reset()
            bB.reset()
            sqs3 = [bB.get(512), bB.get(512)]
            lnv3 = bF.get(512)
            rstd3 = bF.get(512)
            for k in range(4):
                c0 = 128 + 512 * k
                norm(lambda kc, c0=c0: X[:, kc, c0:c0 + 512], 512, V_G + 32,
                     lambda kc, c0=c0: X[:, kc, c0:c0 + 512], xkeys(c0, c0 + 512), xkeys(c0, c0 + 512),
                     banks[k % 4], "B%d" % (k % 4), sqs3, lnv3, rstd3, rstd_psum=RSTD_PSUM)
        dump_x()
        P.emit(final_waits=final_keys)
    nc._knames = P.names
    return nc


def make_consts():
    s = np.arange(128)
    tri = (s[:, None] <= s[None, :]).astype(np.float32)
    cf = np.concatenate([tri, np.ones((128, 128), np.float32)], axis=1)
    cb = np.concatenate([np.eye(128, dtype=np.float32)] + [tri] * 4, axis=1)
    return np.ascontiguousarray(cf), np.ascontiguousarray(cb)


def make_vec(inp, half):
    v = np.zeros((128, NV), np.float32)

    def fm(a, nblk):
        return np.asarray(a, np.float32).reshape(nblk, 128).T

    for i, nm in enumerate(["l0_norm_mix", "l0_norm_ffn", "l1_norm_mix", "l1_norm_ffn", "final_norm"]):
        v[:, V_G + 8 * i:V_G + 8 * i + 8] = fm(inp[nm], 8)
    for l in range(2):
        base = V_FC + l * 176
        cw = np.asarray(inp["l0_ffn_conv_w"] if l == 0 else inp["l1_ffn_conv_w"], np.float32)
        v[:, base:base + 132] = cw.reshape(3, 44, 128).transpose(2, 1, 0).reshape(128, 132)
        v[:, base + 132:base + 176] = fm(inp["l0_ffn_conv_b"] if l == 0 else inp["l1_ffn_conv_b"], 44)
    cw = np.asarray(inp["l1_sconv_conv_w"], np.float32)
    v[:, V_SC:V_SC + 24] = cw.reshape(3, 8, 128).transpose(2, 1, 0).reshape(128, 24)
    v[:, V_SC + 24:V_SC + 32] = fm(inp["l1_sconv_conv_b"], 8)
    v[:, V_HN:V_HN + 8] = fm(np.asarray(inp["l0_mlstm_head_norm"], np.float32).reshape(-1), 8)
    v[:, V_BG:V_BG + 8] = np.asarray(inp["l0_mlstm_b_gates"], np.float32)[None, :]
    v[:, V_FLAG] = float(half)
    return v


_CACHE = {}


def kernel(**inputs):
    upto = 5
    if "nc" not in _CACHE:
        _CACHE["nc"] = build_program(upto)
    nc = _CACHE["nc"]
    x = np.asarray(inputs["x"], np.float32)
    cf, cb = make_consts()
    wnames = ["l0_w_in", "l0_w_out", "l0_w_up", "l0_w_down", "l1_w_in", "l1_w_out", "l1_w_up", "l1_w_down"]
    src = {"l0_w_in": "l0_mlstm_w_in", "l0_w_out": "l0_mlstm_w_out", "l0_w_up": "l0_ffn_w_up", "l0_w_down": "l0_ffn_w_down",
           "l1_w_in": "l1_sconv_w_in", "l1_w_out": "l1_sconv_w_out", "l1_w_up": "l1_ffn_w_up", "l1_w_down": "l1_ffn_w_down"}
    wts = {k: np.ascontiguousarray(np.asarray(inputs[src[k]], np.float32)) for k in wnames}
    in_maps = []
    for c in range(NCORES):
        b, half = c // 2, c % 2
        xt = x[b].T
        xm = np.zeros((D, XW), np.float32)
        xp = np.zeros((D, NPRE), np.float32)
        if half == 0:
            xm[:, HALO:] = xt[:, 0:NTOK]
        else:
            xm[:, :] = xt[:, NTOK - HALO:2 * NTOK]
            xp[:, :] = xt[:, 0:NPRE]
        m = dict(xm=np.ascontiguousarray(xm), xp=np.ascontiguousarray(xp), vec=make_vec(inputs, half), cf=cf, cb=cb)
        m.update(wts)
        in_maps.append(m)
    res = run_bass_kernel_spmd(nc, in_maps, core_ids=list(range(NCORES)))
    out = np.empty((4, 2 * NTOK, D), np.float32)
    for c in range(NCORES):
        b, half = c // 2, c % 2
        out[b, half * NTOK:(half + 1) * NTOK, :] = np.asarray(res.results[c]["y"]).T
    return out
```

```python
import math
import numpy as np
from contextlib import ExitStack
import concourse.bass as bass
import concourse.mybir as mybir
from concourse.bass_utils import run_bass_kernel_spmd

F32 = mybir.dt.float32
BF16 = mybir.dt.bfloat16
AF = mybir.ActivationFunctionType
ALU = mybir.AluOpType

NCORES = 8
D = 1024
KC = 8
NTOK = 2048
HALO = 128
XW = HALO + NTOK
NPRE = 1920
DFF = 2816
NFB = 22
EPS = 1e-6
SLOT = 16448
FFN_GROUPS = [(0, 5), (5, 10), (10, 14), (14, 18), (18, 22)]
SC_GROUPS = [(0, 4), (4, 8)]
OUT_RANGES = [(124, 535), (535, 946), (946, 1357), (1357, 1768), (1768, 2176)]
XN0 = 120
XNW = XW - XN0

V_G = 0
V_FC = 40
V_SC = V_FC + 352
V_HN = V_SC + 32
V_BG = V_HN + 8
V_FLAG = V_BG + 8
NV = V_FLAG + 1


class Instr:
    __slots__ = ("eng", "fn", "deps", "signals", "sigval", "is_dma", "sem", "semval", "idx", "semkey", "lbl")


class Prog:
    ENGS = ("pe", "act", "dve", "pool", "sp")

    def __init__(self, nc, stack):
        self.nc = nc
        self.stack = stack
        self.lists = {e: [] for e in self.ENGS}
        self.last_w = {}
        self.readers = {}
        self.dma_sems = {}
        self.engsem = {}
        self.fence = []
        self.fenced = {e: True for e in self.ENGS}
        self.lbl = ""
        self.names = {}

    def _sem(self, name):
        return self.stack.enter_context(self.nc.semaphore(name))

    def set_fence(self):
        self.fence = []
        for e in self.ENGS:
            if self.lists[e]:
                self.fence.append(self.lists[e][-1])
            self.fenced[e] = False

    def add(self, eng, fn, reads=(), writes=(), dma_sem=None):
        ins = Instr()
        ins.eng = eng
        ins.fn = fn
        ins.signals = False
        ins.sigval = 0
        ins.is_dma = dma_sem is not None
        ins.idx = len(self.lists[eng])
        ins.sem = None
        ins.semval = 0
        ins.semkey = dma_sem
        ins.lbl = self.lbl
        deps = []
        seen = set()

        def dep(d, force=False):
            if d is None or id(d) in seen:
                return
            seen.add(id(d))
            if (not force) and (not d.is_dma) and d.eng == eng and not ins.is_dma:
                if eng == "pe":
                    return
                if ins.idx - d.idx > 3:
                    return
            if d.is_dma and ins.is_dma and d.semkey == ins.semkey:
                return
            deps.append(d)
            if not d.is_dma:
                d.signals = True

        for k in reads:
            dep(self.last_w.get(k))
        for k in writes:
            dep(self.last_w.get(k))
            rd = self.readers.get(k)
            if rd:
                for r in rd.values():
                    dep(r)
        if not self.fenced[eng] and eng in ("pe", "act", "dve"):
            self.fenced[eng] = True
            for f in self.fence:
                if f.eng != eng or f.is_dma:
                    dep(f, force=True)
        ins.deps = deps
        if ins.is_dma:
            if dma_sem not in self.dma_sems:
                self.dma_sems[dma_sem] = [self._sem("d_" + str(dma_sem)), 0]
            ent = self.dma_sems[dma_sem]
            ent[1] += 16
            ins.sem = ent[0]
            ins.semval = ent[1]
        for k in reads:
            rd = self.readers.setdefault(k, {})
            rd[("dma", id(ins)) if ins.is_dma else eng] = ins
        for k in writes:
            self.last_w[k] = ins
            self.readers[k] = {}
        self.lists[eng].append(ins)
        return ins

    def emit(self, final_waits=()):
        nc = self.nc
        for e in self.ENGS:
            self.engsem[e] = self._sem("e_" + e)
            cnt = 0
            for ins in self.lists[e]:
                if ins.signals and not ins.is_dma:
                    cnt += 1
                    ins.sigval = cnt
        engsem = self.engsem
        lists = self.lists

        def run(e, handle):
            waited = {}
            for ins in lists[e]:
                needs = {}
                for d in ins.deps:
                    if d.is_dma:
                        s, v = d.sem, d.semval
                    else:
                        s, v = engsem[d.eng], d.sigval
                    key = id(s)
                    if key not in needs or needs[key][1] < v:
                        needs[key] = (s, v)
                for key, (s, v) in needs.items():
                    if waited.get(key, 0) < v:
                        handle.wait_ge(s, v)
                        waited[key] = v
                inst = ins.fn(handle)
                try:
                    self.names[str(inst.ins.name)] = ins.lbl
                except Exception:
                    pass
                if ins.is_dma:
                    inst.then_inc(ins.sem, 16)
                elif ins.signals:
                    inst.then_inc(engsem[e], 1)
            if e == "sp":
                for k in final_waits:
                    ent = self.dma_sems[k]
                    handle.wait_ge(ent[0], ent[1])

        with nc.Block() as block:
            @block.tensor
            def _(h):
                run("pe", h)

            @block.scalar
            def _(h):
                run("act", h)

            @block.vector
            def _(h):
                run("dve", h)

            @block.gpsimd
            def _(h):
                run("pool", h)

            @block.sync
            def _(h):
                run("sp", h)


def xkeys(c0, c1):
    return ["X%d" % c for c in range(c0 // 128, (c1 - 1) // 128 + 1)]


def build_program(upto=5):
    nc = bass.Bass("TRN2", target_bir_lowering=False)
    dt_in = lambda n, s: nc.dram_tensor(n, s, F32, kind="ExternalInput").ap()
    xm = dt_in("xm", [D, XW])
    xp = dt_in("xp", [D, NPRE])
    vec_d = dt_in("vec", [128, NV])
    cf_d = dt_in("cf", [128, 256])
    cb_d = dt_in("cb", [128, 128 + 512])
    w0_in = dt_in("l0_w_in", [D, 3080])
    w0_out = dt_in("l0_w_out", [D, D])
    w_up = [dt_in("l0_w_up", [D, 2 * DFF]), dt_in("l1_w_up", [D, 2 * DFF])]
    w_dn = [dt_in("l0_w_down", [DFF, D]), dt_in("l1_w_down", [DFF, D])]
    w1_in = dt_in("l1_w_in", [D, 3 * D])
    w1_out = dt_in("l1_w_out", [D, D])
    y = nc.dram_tensor("y", [D, NTOK], F32, kind="ExternalOutput").ap()
    import os
    DBG = os.environ.get("K_DBG", "0") == "1"
    if DBG:
        dbg_d = nc.dram_tensor("dbg", [128, 2048], F32, kind="ExternalOutput").ap()

    kview = lambda w: w.rearrange("(kc p) n -> p kc n", p=128)

    with ExitStack() as st:
        P = Prog(nc, st)
        sb = lambda n, s, d: st.enter_context(nc.sbuf_tensor(n, s, d))
        X = sb("X", [128, KC, XW], F32)
        RA = sb("RA", [128, 2 * SLOT], BF16)
        CX = sb("CX", [128, KC * XNW], BF16)
        SCF = sb("SCF", [128, 3550], F32)
        SCB = sb("SCB", [128, 12600], BF16)
        vec = sb("vecs", [128, NV], F32)
        cf = sb("cfs", [128, 256], F32)
        cb = sb("cbs", [128, 640], BF16)
        ones_bf = sb("ones_bf", [128, 128], BF16)
        banks = [st.enter_context(nc.psum_tensor("B%d" % i, [128, 512], F32)) for i in range(8)]
        B5bf = banks[5][:].bitcast(BF16)

        tri = cf[:, 0:128]
        ones_f = cf[:, 128:256]
        ident = cb[:, 0:128]
        mask4 = cb[:, 128:640]

        def MM(out, lhsT, rhs, start, stop, reads, writes):
            P.add("pe", lambda e: e.matmul(out, lhsT=lhsT, rhs=rhs, start=start, stop=stop), reads, writes)

        def TR(out, in_, reads, writes):
            P.add("pe", lambda e: e.transpose(out, in_, ident), list(reads) + ["cb"], writes)

        def ACT(out, in_, func, reads, writes, scale=None, bias=None, accum_out=None):
            kw = {}
            if scale is not None:
                kw["scale"] = scale
            if bias is not None:
                kw["bias"] = bias
            if accum_out is not None:
                kw["accum_out"] = accum_out
            P.add("act", lambda e: e.activation(out=out, in_=in_, func=func, **kw), reads, writes)

        def TT(out, in0, in1, op, reads, writes):
            P.add("dve", lambda e: e.tensor_tensor(out=out, in0=in0, in1=in1, op=op), reads, writes)

        def TS(out, in0, s1, op0, reads, writes, s2=None, op1=None):
            if op1 is None:
                P.add("dve", lambda e: e.tensor_scalar(out=out, in0=in0, scalar1=s1, scalar2=None, op0=op0), reads, writes)
            else:
                P.add("dve", lambda e: e.tensor_scalar(out=out, in0=in0, scalar1=s1, scalar2=s2, op0=op0, op1=op1), reads, writes)

        def STT(out, in0, scalar, in1, op0, op1, reads, writes):
            P.add("dve", lambda e: e.scalar_tensor_tensor(out=out, in0=in0, scalar=scalar, in1=in1, op0=op0, op1=op1), reads, writes)

        def DMA(eng, out, in_, reads, writes, sem):
            P.add(eng, lambda e: e.dma_start(out=out, in_=in_), reads, writes, dma_sem=sem)

        class Bump:
            def __init__(self, t, n):
                self.t, self.n, self.o = t, n, 0

            def reset(self):
                self.o = 0

            def get(self, n):
                n2 = (n + 1) // 2 * 2
                assert self.o + n2 <= self.n, (self.o, n2, self.n)
                ap = self.t[:, self.o:self.o + n]
                self.o += n2
                return ap

        bF = Bump(SCF, 3550)
        bB = Bump(SCB, 12600)

        DMA("sp", vec[:], vec_d, [], ["vec"], "vec")
        DMA("sp", cf[:], cf_d, [], ["cf"], "cf")
        DMA("pool", cb[:], cb_d, [], ["cb"], "cb")
        P.add("dve", lambda e: e.memset(ones_bf[:], 1.0), [], ["ones_bf"])

        slot = [RA[:, 0:SLOT], RA[:, SLOT:2 * SLOT]]
        s0 = slot[0].rearrange("p (k n) -> p k n", k=KC)
        Wqkvg = s0
        Wo = slot[1][:, 0:8192].rearrange("p (k n) -> p k n", k=KC)
        Wout0 = slot[1][:, 8192:16384].rearrange("p (k n) -> p k n", k=KC)
        w0v = kview(w0_in)
        DMA("pool", Wqkvg[:, :, 2048:2056], w0v[:, :, 3072:3080], [], ["w_g"], "w_g")
        DMA("pool", Wqkvg[:, :, 512:1024], w0v[:, :, 512:1024], [], ["w_k"], "w_k")
        DMA("pool", Wqkvg[:, :, 1024:2048], w0v[:, :, 1024:2048], [], ["w_v"], "w_v")
        DMA("pool", Wqkvg[:, :, 0:512], w0v[:, :, 0:512], [], ["w_q"], "w_q")
        DMA("pool", Wo, w0v[:, :, 2048:3072], [], ["slot1a"], "slot1a")
        DMA("pool", Wout0, kview(w0_out), [], ["slot1b"], "slot1b")

        def norm(src_fn, n, gcol, dst_fn, rkeys, wkeys, ssbank, sskey, sqs, lnv, rstd, rstd_psum=False):
            for kc in range(KC):
                sq = sqs[kc % 2][:, 0:n]
                ACT(sq, src_fn(kc), AF.Square, rkeys, ["sq%d" % (kc % 2)])
                MM(ssbank[:, 0:n], ones_bf[:], sq, kc == 0, kc == KC - 1, ["sq%d" % (kc % 2), "ones_bf"], [sskey])
            ACT(lnv[:, 0:n], ssbank[:, 0:n], AF.Ln, [sskey], ["lnv"], scale=1.0 / D, bias=EPS)
            if rstd_psum:
                ACT(ssbank[:, 0:n], lnv[:, 0:n], AF.Exp, ["lnv"], [sskey], scale=-0.5)
                rs_ap, rs_key = ssbank[:, 0:n], sskey
            else:
                ACT(rstd[:, 0:n], lnv[:, 0:n], AF.Exp, ["lnv"], ["rstd"], scale=-0.5)
                rs_ap, rs_key = rstd[:, 0:n], "rstd"
            for kc in range(KC):
                STT(dst_fn(kc), src_fn(kc), vec[:, gcol + kc:gcol + kc + 1], rs_ap, ALU.mult, ALU.mult,
                    list(rkeys) + [rs_key, "vec"], wkeys)

        sqs = [bB.get(512), bB.get(512)]
        lnv = bF.get(512)
        rstd = bF.get(512)
        ogtmp = bF.get(512)
        ntmp = bF.get(512)
        C_f = bF.get(1024)
        n_f = bF.get(4)
        small = [dict(gl=bF.get(8), e4=bF.get(4), l4=bF.get(4), A=bF.get(16), E=bF.get(16)) for _ in range(2)]
        dsb = bF.get(4)
        dab = bF.get(4)
        rden = bF.get(4)
        ssr = bF.get(4)
        t1 = bF.get(4)
        t2 = bF.get(4)
        scl = bF.get(4)
        tok = [dict(qs=bB.get(512), ks=bB.get(512), kw=bB.get(512), v=bB.get(1024)) for _ in range(2)]
        qkT = [bB.get(1024), bB.get(1024)]
        SmT = [bB.get(512), bB.get(512)]
        hst = [bB.get(1024), bB.get(1024)]
        C_bf = bB.get(1024)
        n_bf = bB.get(4)
        junk = bB.get(256)
        xnb = [CX[:, 0:4096].rearrange("p (k n) -> p k n", k=KC), CX[:, 4096:8192].rearrange("p (k n) -> p k n", k=KC)]
        ogT = CX[:, 8192:12288].rearrange("p (k n) -> p k n", k=KC)
        hsT = CX[:, 12288:16384].rearrange("p (k n) -> p k n", k=KC)

        P.add("dve", lambda e: e.memset(C_f, 0.0), [], ["Cf"])
        P.add("dve", lambda e: e.memset(n_f, 0.0), [], ["nf"])
        P.add("dve", lambda e: e.memset(C_bf, 0.0), [], ["Cbf"])
        P.add("dve", lambda e: e.memset(n_bf, 0.0), [], ["nbf"])

        groups = []
        for j, (t0, nt) in enumerate([(0, 512), (512, 512), (1024, 512), (1536, 384)]):
            groups.append(dict(xc=128 + 512 * (j % 2), n=nt, full=False, src=xp, sc=t0))
        groups.append(dict(xc=0, n=128, full=True, src=xm, sc=0))
        for k in range(4):
            groups.append(dict(xc=128 + 512 * k, n=512, full=True, src=xm, sc=128 + 512 * k))
        xmv = kview(xm)
        xpv = kview(xp)

        def load_group(gi):
            g = groups[gi]
            v = xpv if g["src"] is xp else xmv
            ks = xkeys(g["xc"], g["xc"] + g["n"])
            DMA("sp", X[:, :, g["xc"]:g["xc"] + g["n"]], v[:, :, g["sc"]:g["sc"] + g["n"]], [], ks, "xld%d" % gi)

        for gi in (0, 1, 4, 7, 8):
            load_group(gi)
        LOAD_AFTER_NORM = {0: 2, 1: 3, 2: 5, 3: 6}

        chunks = []
        for gi, g in enumerate(groups):
            nch = g["n"] // 128
            for c in range(nch):
                chunks.append(dict(g=gi, c=c, first=(c == 0), last=(c == nch - 1), full=g["full"]))

        LNSCALE = math.log(128.0 ** -0.5)

        def do_norm(gi):
            P.lbl = "do_norm(%d)" % gi
            g = groups[gi]
            xc, n = g["xc"], g["n"]
            xn = xnb[gi % 2]
            norm(lambda kc: X[:, kc, xc:xc + n], n, V_G + 0, lambda kc: xn[:, kc, 0:n],
                 xkeys(xc, xc + n), ["xn%d" % (gi % 2)], banks[6], "B6", sqs, lnv, rstd)

        def do_ogate(gi):
            P.lbl = "do_ogate(%d)" % gi
            g = groups[gi]
            n = g["n"]
            xn = xnb[gi % 2]
            for vb in range(8):
                bk = (6, 7, 3, 4)[vb % 4]
                for kc in range(KC):
                    MM(banks[bk][:, 0:n], Wo[:, kc, vb * 128:(vb + 1) * 128], xn[:, kc, 0:n], kc == 0, kc == KC - 1,
                       ["slot1a", "xn%d" % (gi % 2)], ["B%d" % bk])
                if USE_SIGMOID:
                    ACT(ogtmp[:, 0:n], banks[bk][:, 0:n], AF.Sigmoid, ["B%d" % bk], ["ogtmp"])
                else:
                    ACT(ogtmp[:, 0:n], banks[bk][:, 0:n], AF.Exp, ["B%d" % bk], ["ogtmp"], scale=-1.0)
                    ACT(ogtmp[:, 0:n], ogtmp[:, 0:n], AF.Ln, ["ogtmp"], ["ogtmp"], bias=1.0)
                    ACT(ogtmp[:, 0:n], ogtmp[:, 0:n], AF.Exp, ["ogtmp"], ["ogtmp"], scale=-1.0)
                TS(ogT[:, vb, 0:n], ogtmp[:, 0:n], vec[:, V_HN + vb:V_HN + vb + 1], ALU.mult, ["ogtmp", "vec"], ["ogT"])

        def do_wout(gi):
            P.lbl = "do_wout(%d)" % gi
            g = groups[gi]
            xc, n = g["xc"], g["n"]
            for dc in range(8):
                bk = (6, 7, 3, 4)[dc % 4]
                for kc in range(KC):
                    MM(banks[bk][:, 0:n], Wout0[:, kc, dc * 128:(dc + 1) * 128], hsT[:, kc, 0:n], kc == 0, kc == KC - 1,
                       ["slot1b", "hsT"], ["B%d" % bk])
                TT(X[:, dc, xc:xc + n], X[:, dc, xc:xc + n], banks[bk][:, 0:n], ALU.add,
                   ["B%d" % bk] + xkeys(xc, xc + n), xkeys(xc, xc + n))

        def A_gates(i):
            P.lbl = "A_gates(%d)" % i
            ch = chunks[i]
            par = i % 2
            xn = xnb[ch["g"] % 2]
            xk = "xn%d" % (ch["g"] % 2)
            cs = ch["c"] * 128
            sm = small[par]
            for kc in range(KC):
                MM(banks[0][:, 0:8], xn[:, kc, cs:cs + 128], Wqkvg[:, kc, 2048:2056], kc == 0, kc == KC - 1,
                   [xk, "w_g"], ["B0"])
            TT(sm["gl"], banks[0][:, 0:8], vec[:, V_BG:V_BG + 8], ALU.add, ["B0", "vec"], ["gl%d" % par])
            ACT(sm["e4"], sm["gl"][:, 4:8], AF.Exp, ["gl%d" % par], ["e4%d" % par], scale=-1.0)
            ACT(sm["l4"], sm["e4"], AF.Ln, ["e4%d" % par], ["l4%d" % par], bias=1.0)

        def A_cs(i):
            P.lbl = "A_cs(%d)" % i
            par = i % 2
            sm = small[par]
            MM(banks[0][:, 8:12], tri, sm["l4"], True, True, ["cf", "l4%d" % par], ["B0"])
            MM(banks[0][:, 12:16], ones_f, sm["l4"], True, True, ["cf", "l4%d" % par], ["B0"])
            A = sm["A"]
            TT(A[:, 4:8], sm["gl"][:, 0:4], banks[0][:, 8:12], ALU.add, ["gl%d" % par, "B0"], ["A%d" % par])
            TT(A[:, 8:12], A[:, 4:8], banks[0][:, 12:16], ALU.subtract, ["A%d" % par, "B0"], ["A%d" % par])
            TS(A[:, 0:4], banks[0][:, 8:12], -1.0, ALU.mult, ["B0"], ["A%d" % par], s2=LNSCALE, op1=ALU.add)
            TS(A[:, 12:16], banks[0][:, 12:16], -1.0, ALU.mult, ["B0"], ["A%d" % par])
            ACT(sm["E"], A, AF.Exp, ["A%d" % par], ["E%d" % par])

        def bc4(ap4, w):
            return ap4.unsqueeze(2).to_broadcast([128, 4, w])

        def v3(ap, k):
            return ap.rearrange("p (h d) -> p h d", h=k)

        def A_qk(i, which="qk"):
            P.lbl = "A_qk(%d)" % i
            ch = chunks[i]
            par = i % 2
            xn = xnb[ch["g"] % 2]
            xk = "xn%d" % (ch["g"] % 2)
            cs = ch["c"] * 128
            if ch["full"] and which in ("q", "qk"):
                for kc in range(KC):
                    MM(banks[1][:], xn[:, kc, cs:cs + 128], Wqkvg[:, kc, 0:512], kc == 0, kc == KC - 1, [xk, "w_q"], ["B1"])
            if which in ("k", "qk"):
                for kc in range(KC):
                    MM(banks[2][:], xn[:, kc, cs:cs + 128], Wqkvg[:, kc, 512:1024], kc == 0, kc == KC - 1, [xk, "w_k"], ["B2"])

        def A_qk_evac(i):
            P.lbl = "A_qk_evac(%d)" % i
            ch = chunks[i]
            par = i % 2
            E = small[par]["E"]
            tk = tok[par]
            if ch["full"]:
                TT(v3(tk["qs"], 4), v3(banks[1][:], 4), bc4(E[:, 0:4], 128), ALU.mult, ["B1", "E%d" % par], ["qs%d" % par])
                TT(v3(tk["ks"], 4), v3(banks[2][:], 4), bc4(E[:, 4:8], 128), ALU.mult, ["B2", "E%d" % par], ["ks%d" % par])
            TT(v3(tk["kw"], 4), v3(banks[2][:], 4), bc4(E[:, 8:12], 128), ALU.mult, ["B2", "E%d" % par], ["kw%d" % par])

        normed = set()
        pending_stt = []
        USE_SIGMOID = os.environ.get("K_SIGMOID", "1") == "1"
        USE_POOL_SQ = os.environ.get("K_POOLSQ", "1") == "1"

        def flush_stt(k):
            while pending_stt and k > 0:
                pending_stt.pop(0)[1]()
                k -= 1

        def A_v(i, norm_gi=None, halves=(0, 1)):
            P.lbl = "A_v(%d)" % i
            ch = chunks[i]
            par = i % 2
            xn = xnb[ch["g"] % 2]
            xk = "xn%d" % (ch["g"] % 2)
            cs = ch["c"] * 128
            tk = tok[par]
            pairs = []
            pairs_made = False
            if norm_gi is not None and norm_gi not in normed:
                pairs_made = True
                flush_stt(len(pending_stt))
                normed.add(norm_gi)
                g2 = groups[norm_gi]
                xc2, n2 = g2["xc"], g2["n"]
                xn2 = xnb[norm_gi % 2]
                rk = xkeys(xc2, xc2 + n2)

                def mkpair(kc):
                    def f():
                        lb = P.lbl
                        P.lbl = "normA(%d)" % norm_gi
                        sq = sqs[kc % 2][:, 0:n2]
                        xin = X[:, kc, xc2:xc2 + n2]
                        if USE_POOL_SQ:
                            P.add("pool", lambda e: e.tensor_tensor(out=sq, in0=xin, in1=xin, op=ALU.mult), rk, ["sq%d" % (kc % 2)])
                        else:
                            ACT(sq, xin, AF.Square, rk, ["sq%d" % (kc % 2)])
                        MM(banks[5][:, 0:n2], ones_bf[:], sq, kc == 0, kc == KC - 1, ["sq%d" % (kc % 2), "ones_bf"], ["B5"])
                        P.lbl = lb
                    return f
                pairs = [mkpair(kc) for kc in range(KC)]

                def fin():
                    P.lbl = "normA(%d)" % norm_gi
                    ACT(lnv[:, 0:n2], banks[5][:, 0:n2], AF.Ln, ["B5"], ["lnv"], scale=1.0 / D, bias=EPS)
                    ACT(rstd[:, 0:n2], lnv[:, 0:n2], AF.Exp, ["lnv"], ["rstd"], scale=-0.5)

                def mkstt(kc):
                    def f():
                        P.lbl = "normB(%d)" % norm_gi
                        STT(xn2[:, kc, 0:n2], X[:, kc, xc2:xc2 + n2], vec[:, V_G + kc:V_G + kc + 1], rstd[:, 0:n2],
                            ALU.mult, ALU.mult, rk + ["rstd", "vec"], ["xn%d" % (norm_gi % 2)])
                        if kc == KC - 1 and norm_gi in LOAD_AFTER_NORM:
                            load_group(LOAD_AFTER_NORM[norm_gi])
                    return f
                for kc in range(KC):
                    pending_stt.append((norm_gi, mkstt(kc)))
            for hh in halves:
                for kc in range(KC):
                    MM(banks[3 + hh][:], xn[:, kc, cs:cs + 128], Wqkvg[:, kc, 1024 + 512 * hh:1536 + 512 * hh],
                       kc == 0, kc == KC - 1, [xk, "w_v"], ["B%d" % (3 + hh)])
                    if pairs and (len(halves) == 1 or kc % 2 == 1):
                        pairs.pop(0)()
                ACT(tk["v"][:, 512 * hh:512 * hh + 512], banks[3 + hh][:], AF.Copy, ["B%d" % (3 + hh)], ["v%d_%d" % (par, hh)])
            if pairs_made:
                while pairs:
                    pairs.pop(0)()
                fin()

        def B_T(i):
            P.lbl = "B_T(%d)" % i
            par = i % 2
            tk = tok[par]
            for h in range(4):
                TR(B5bf[:, h * 128:(h + 1) * 128], tk["qs"][:, h * 128:(h + 1) * 128], ["qs%d" % par], ["B5"])
            for h in range(4):
                TR(B5bf[:, 512 + h * 128:512 + (h + 1) * 128], tk["ks"][:, h * 128:(h + 1) * 128], ["ks%d" % par], ["B5"])
            ACT(qkT[par], B5bf[:, 0:1024], AF.Copy, ["B5"], ["qkT%d" % par])

        def B_S(i):
            P.lbl = "B_S(%d)" % i
            par = i % 2
            for h in range(4):
                MM(banks[5][:, h * 128:(h + 1) * 128], qkT[par][:, 512 + h * 128:512 + (h + 1) * 128],
                   qkT[par][:, h * 128:(h + 1) * 128], True, True, ["qkT%d" % par], ["B5"])
            TT(SmT[par], banks[5][:], mask4, ALU.mult, ["B5", "cb"], ["SmT%d" % par])

        def B_num(i):
            P.lbl = "B_num(%d)" % i
            par = i % 2
            tk = tok[par]
            for h in range(4):
                bk = 6 + h // 2
                o = banks[bk][:, (h % 2) * 256:(h % 2) * 256 + 256]
                MM(o, SmT[par][:, h * 128:(h + 1) * 128], tk["v"][:, h * 256:(h + 1) * 256], True, False,
                   ["SmT%d" % par, "v%d_%d" % (par, h // 2)], ["B%d" % bk])
                MM(o, qkT[par][:, h * 128:(h + 1) * 128], C_bf[:, h * 256:(h + 1) * 256], False, True,
                   ["qkT%d" % par, "Cbf"], ["B%d" % bk])
            for h in range(4):
                o = banks[0][:, 16 + h:17 + h]
                MM(o, SmT[par][:, h * 128:(h + 1) * 128], ones_bf[:, 0:1], True, False, ["SmT%d" % par, "ones_bf"], ["B0"])
                MM(o, qkT[par][:, h * 128:(h + 1) * 128], n_bf[:, h:h + 1], False, True, ["qkT%d" % par, "nbf"], ["B0"])

        def B_hs(i):
            P.lbl = "B_hs(%d)" % i
            par = i % 2
            for h in range(4):
                bk = 6 + h // 2
                ACT(junk, banks[bk][:, (h % 2) * 256:(h % 2) * 256 + 256], AF.Square, ["B%d" % bk], ["junk", "ssr"],
                    accum_out=ssr[:, h:h + 1])
            TS(dsb, banks[0][:, 16:20], -1.0, ALU.mult, ["B0"], ["dsb"])
            TT(dab, dsb, banks[0][:, 16:20], ALU.max, ["dsb", "B0"], ["dab"])
            TS(dab, dab, 1.0, ALU.max, ["dab"], ["dab"])
            TT(t1, dab, dab, ALU.mult, ["dab"], ["t1"])
            STT(t2, t1, 256.0 * EPS, ssr, ALU.mult, ALU.add, ["t1", "ssr"], ["t2"])
            ACT(t1, t2, AF.Ln, ["t2"], ["t1"], scale=1.0 / 256.0)
            ACT(scl, t1, AF.Exp, ["t1"], ["scl"], scale=-0.5)
            for h in range(4):
                bk = 6 + h // 2
                src = banks[bk][:, (h % 2) * 256:(h % 2) * 256 + 256]
                dst = hst[par][:, h * 256:(h + 1) * 256]
                ACT(dst, src, AF.Identity, ["B%d" % bk, "scl"], ["hst%d_%d" % (par, h)], scale=scl[:, h:h + 1])

        def B_dC(i):
            P.lbl = "B_dC(%d)" % i
            par = i % 2
            tk = tok[par]
            E = small[par]["E"]
            bb0 = 1 if chunks[i]["full"] else 6
            for h in range(4):
                bk = bb0 + h // 2
                MM(banks[bk][:, (h % 2) * 256:(h % 2) * 256 + 256], tk["kw"][:, h * 128:(h + 1) * 128],
                   tk["v"][:, h * 256:(h + 1) * 256], True, True, ["kw%d" % par, "v%d_%d" % (par, h // 2)], ["B%d" % bk])
            for h in range(4):
                MM(banks[0][:, 20 + h:21 + h], tk["kw"][:, h * 128:(h + 1) * 128], ones_bf[:, 0:1], True, True,
                   ["kw%d" % par, "ones_bf"], ["B0"])
            TT(n_f, n_f, E[:, 12:16], ALU.mult, ["nf", "E%d" % par], ["nf"])
            TT(n_f, n_f, banks[0][:, 20:24], ALU.add, ["nf", "B0"], ["nf"])
            for h in range(4):
                bk = bb0 + h // 2
                STT(C_f[:, h * 256:(h + 1) * 256], C_f[:, h * 256:(h + 1) * 256], E[:, 12 + h:13 + h],
                    banks[bk][:, (h % 2) * 256:(h % 2) * 256 + 256], ALU.mult, ALU.add,
                    ["Cf", "E%d" % par, "B%d" % bk], ["Cf"])
            ACT(C_bf, C_f, AF.Copy, ["Cf"], ["Cbf"])
            ACT(n_bf, n_f, AF.Copy, ["nf"], ["nbf"])

        B3bf = banks[3][:].bitcast(BF16)

        def B_hT(i, late=False):
            P.lbl = "B_hT(%d)" % i
            ch = chunks[i]
            par = i % 2
            cs = ch["c"] * 128
            pb_, pk_ = (B3bf, "B3") if late else (B5bf, "B5")
            for vb in range(8):
                TR(pb_[:, vb * 128:(vb + 1) * 128], hst[par][:, vb * 128:(vb + 1) * 128], ["hst%d_%d" % (par, vb // 2)], [pk_])
            TT(hsT[:, :, cs:cs + 128], v3(pb_[:, 0:1024], 8), ogT[:, :, cs:cs + 128], ALU.mult, [pk_, "ogT"], ["hsT"])

        def ensure_norm(gi):
            if gi not in normed:
                normed.add(gi)
                do_norm(gi)
                if gi in LOAD_AFTER_NORM:
                    load_group(LOAD_AFTER_NORM[gi])

        def A_all_first(i):
            ensure_norm(chunks[i]["g"])
            A_gates(i)
            A_cs(i)
            A_qk(i)
            A_qk_evac(i)
            A_v(i)

        A_all_first(0)
        nchunks = len(chunks)
        carrier = {1: 1}
        late_hT = [None]
        LATE_HT = os.environ.get("K_LATEHT", "1") == "1"
        for G in range(2, len(groups)):
            cs_ = [ci for ci, c in enumerate(chunks) if c["g"] == G - 2]
            carrier[max(max(cs_), min(cs_) + 1)] = G
        import os
        FENCE_IT = os.environ.get("K_FENCE", "0") == "1"
        for i in range(nchunks):
            ch = chunks[i]
            nx = i + 1 if i + 1 < nchunks else None
            if FENCE_IT:
                P.set_fence()
            if i >= int(os.environ.get("K_STOPIT", "999")):
                break
            ngi = None
            if nx is not None:
                ensure_norm(chunks[nx]["g"])
                ngi = carrier.get(nx)
                A_gates(nx)
                A_qk(nx, "q")
            if late_hT[0] is not None:
                B_hT(late_hT[0], late=True)
                late_hT[0] = None
            if ch["full"]:
                B_T(i)
                if ch["first"]:
                    do_ogate(ch["g"])
            if nx is not None:
                A_qk(nx, "k")
            if ch["full"]:
                B_S(i)
            if nx is not None:
                A_cs(nx)
                A_qk_evac(nx)
            flush_stt(2)
            if ch["full"]:
                if nx is not None:
                    A_v(nx, None, halves=(0,))
                B_num(i)
                B_hs(i)
                if nx is not None:
                    A_v(nx, ngi, halves=(1,))
                B_dC(i)
            else:
                B_dC(i)
                if nx is not None:
                    A_v(nx, ngi)
            need_all = (i + 2 < nchunks and chunks[i + 2]["first"] and any(g_ == chunks[i + 2]["g"] for g_, _ in pending_stt))
            flush_stt(len(pending_stt) if need_all else 2)
            if ch["full"]:
                if ch["last"] or nx is None or not LATE_HT:
                    B_hT(i)
                    if ch["last"]:
                        do_wout(ch["g"])
                else:
                    late_hT[0] = i

        flush_stt(len(pending_stt))
        if DBG and upto == 1:
            DMA("sp", dbg_d[:, 0:1024].rearrange("p (k n) -> p k n", k=KC), X[:, :, 0:128], ["X0"], [], "dbgout")
            DMA("sp", dbg_d[:, 1024:2048], C_f, ["Cf"], [], "dbgout")

        def dump_x():
            yv = kview(y)
            for k in range(4):
                c0 = 128 + 512 * k
                DMA("sp", yv[:, :, 512 * k:512 * k + 512], X[:, :, c0:c0 + 512], xkeys(c0, c0 + 512), [], "yout")

        XN = CX[:, 0:KC * XNW].rearrange("p (k n) -> p k n", k=KC)
        NORM_RANGES = [(120, 535)] + OUT_RANGES[1:]
        TW = 416

        def proj_phase(kind, layer, gcol, first_xn):
            P.set_fence()
            bF.reset()
            bB.reset()
            sqs2 = [bB.get(512), bB.get(512)]
            lnv2 = bF.get(512)
            rstd2 = bF.get(512)
            if kind == "ffn":
                ag = [bF.get(TW), bF.get(TW)]
                av = [bF.get(TW), bF.get(TW)]
                sg = [bB.get(TW), bB.get(TW)]
                tp1 = [bF.get(TW), bF.get(TW)]
                NBMAX = 5
                grp = FFN_GROUPS
            else:
                tc_ = [bF.get(TW), bF.get(TW)]
                cx = [bF.get(TW), bF.get(TW)]
                yy = [bF.get(TW), bF.get(TW)]
                NBMAX = 4
                grp = SC_GROUPS
            H = [bB.get(NBMAX * TW).rearrange("p (j n) -> p j n", j=NBMAX) for _ in range(2)]
            for k, (a, b) in enumerate(NORM_RANGES):
                wk = ["xn_t%d" % k] + (["xn0", "xn1", "ogT", "hsT"] if first_xn else [])
                norm(lambda kc, a=a, b=b: X[:, kc, a:b], b - a, gcol,
                     lambda kc, a=a, b=b: XN[:, kc, a - XN0:b - XN0], xkeys(a, b), wk, banks[6 + k % 2], "B%d" % (6 + k % 2),
                     sqs2, lnv2, rstd2, rstd_psum=RSTD_PSUM)
            gstate = {}

            def load_w(gidx):
                b0, b1 = grp[gidx]
                nb = b1 - b0
                sl = gstate["slotbase"] + gidx
                s = slot[sl % 2]
                sk = ["slot0", "w_g", "w_k", "w_v", "w_q"] if sl % 2 == 0 else ["slot1a", "slot1b"]
                sem = "slot%d" % (sl % 2)
                if kind == "ffn":
                    wu = kview(w_up[layer])
                    g_ap = s[:, 0:KC * nb * 128].rearrange("p (k n) -> p k n", k=KC)
                    v_ap = s[:, KC * nb * 128:2 * KC * nb * 128].rearrange("p (k n) -> p k n", k=KC)
                    d_ap = s[:, 2 * KC * nb * 128:2 * KC * nb * 128 + nb * 1024].rearrange("p (j n) -> p j n", j=nb)
                    DMA("pool", g_ap, wu[:, :, b0 * 128:b1 * 128], [], sk, sem)
                    DMA("pool", v_ap, wu[:, :, DFF + b0 * 128:DFF + b1 * 128], [], sk, sem)
                    DMA("pool", d_ap, w_dn[layer][b0 * 128:b1 * 128, :].rearrange("(j p) n -> p j n", p=128), [], sk, sem)
                    return dict(parts=[g_ap, v_ap], down=d_ap, keys=sk, nb=nb, b0=b0)
                else:
                    wi = kview(w1_in)
                    aps = []
                    for part in range(3):
                        ap = s[:, part * KC * nb * 128:(part + 1) * KC * nb * 128].rearrange("p (k n) -> p k n", k=KC)
                        DMA("pool", ap, wi[:, :, part * D + b0 * 128:part * D + b1 * 128], [], sk, sem)
                        aps.append(ap)
                    d_ap = s[:, 3 * KC * nb * 128:3 * KC * nb * 128 + nb * 1024].rearrange("p (j n) -> p j n", j=nb)
                    DMA("pool", d_ap, w1_out[b0 * 128:b1 * 128, :].rearrange("(j p) n -> p j n", p=128), [], sk, sem)
                    return dict(parts=aps, down=d_ap, keys=sk, nb=nb, b0=b0)

            return dict(kind=kind, layer=layer, grp=grp, H=H, load_w=load_w, gstate=gstate, tp1=(tp1 if kind == "ffn" else None),
                        bufs=(ag, av, sg) if kind == "ffn" else (tc_, cx, yy))

        slot_counter = [0]
        USE_POOL_SCTAP = os.environ.get("K_POOLSCTAP", "0") == "1"
        RSTD_PSUM = os.environ.get("K_RSTDPSUM", "0") == "1"
        USE_POOL_H = os.environ.get("K_POOLH", "1") == "1"
        USE_ACT_TAP = os.environ.get("K_ACTTAP", "0") == "1"

        def run_proj(ph, preloaded=None):
            kind, layer, grp, H = ph["kind"], ph["layer"], ph["grp"], ph["H"]
            ph["gstate"]["slotbase"] = slot_counter[0]
            ng = len(grp)
            W = {}
            W[0] = ph["load_w"](0)
            if ng > 1:
                W[1] = ph["load_w"](1)
            cnt = [0]
            nparts = 2 if kind == "ffn" else 3
            upb = [0, 1, 2, 3] if kind == "ffn" else [0, 1, 2, 3, 4, 5]
            dnb = [4, 5, 6, 7] if kind == "ffn" else [5, 6, 7]
            dcnt = [0]

            def up(gidx, k, mid=None):
                P.lbl = "up(%s%d,%d,%d)" % (kind, layer, gidx, k)
                w = W[gidx]
                o0, o1 = OUT_RANGES[k]
                p0, n = o0 - 2, o1 - o0 + 2
                no = n - 2
                hb = H[(gidx * 5 + k) % 2]
                hkey = "H%d" % ((gidx * 5 + k) % 2)
                xkeysr = ["xn_t%d" % k] + (["xn_t%d" % (k - 1)] if k > 0 else [])
                for j in range(w["nb"]):
                    if mid is not None and j == (w["nb"] + 1) // 2:
                        lb_ = P.lbl
                        mid()
                        P.lbl = lb_
                    par = cnt[0] % 2
                    cnt[0] += 1
                    if kind == "ffn":
                        bks = [upb[par * nparts + q] for q in range(nparts)]
                        qorder = range(nparts)
                    else:
                        bks = [4, par, 2 + par]
                        qorder = (1, 2, 0)
                    for q in qorder:
                        for kc in range(KC):
                            MM(banks[bks[q]][:, 0:n], w["parts"][q][:, kc, j * 128:(j + 1) * 128],
                               XN[:, kc, p0 - XN0:p0 - XN0 + n], kc == 0, kc == KC - 1,
                               w["keys"] + xkeysr, ["B%d" % bks[q]])
                    blk = w["b0"] + j
                    if kind == "ffn":
                        ag, av, sg = ph["bufs"]
                        tp1 = ph["tp1"]
                        base = V_FC + layer * 176
                        outs = (ag[par], av[par])
                        for q in range(2):
                            cb_ = blk + q * NFB
                            wc = lambda t, cb_=cb_: vec[:, base + cb_ * 3 + t:base + cb_ * 3 + t + 1]
                            bc = vec[:, base + 132 + cb_:base + 132 + cb_ + 1]
                            pb = banks[bks[q]]
                            nm = ("ag%d" if q == 0 else "av%d") % par
                            ACT(outs[q][:, 0:no], pb[:, 2:n], AF.Identity, ["B%d" % bks[q], "vec"], [nm], scale=wc(2), bias=bc)
                            if USE_ACT_TAP:
                                t1_ = tp1[q][:, 0:no]
                                o_ = outs[q][:, 0:no]
                                ACT(t1_, pb[:, 1:n - 1], AF.Identity, ["B%d" % bks[q], "vec"], ["tp1_%d" % q], scale=wc(1))
                                P.add("pool", lambda e, o_=o_, t1_=t1_: e.tensor_tensor(out=o_, in0=o_, in1=t1_, op=ALU.add),
                                      [nm, "tp1_%d" % q], [nm])
                            else:
                                STT(outs[q][:, 0:no], pb[:, 1:n - 1], wc(1), outs[q][:, 0:no], ALU.mult, ALU.add,
                                    ["B%d" % bks[q], "vec", nm], [nm])
                            STT(outs[q][:, 0:no], pb[:, 0:n - 2], wc(0), outs[q][:, 0:no], ALU.mult, ALU.add,
                                ["B%d" % bks[q], "vec", nm], [nm])
                        ACT(sg[par][:, 0:no], ag[par][:, 0:no], AF.Silu, ["ag%d" % par], ["sg%d" % par])
                        if USE_POOL_H:
                            o_, a_, b_ = hb[:, j, 0:no], sg[par][:, 0:no], av[par][:, 0:no]
                            P.add("pool", lambda e, o_=o_, a_=a_, b_=b_: e.tensor_tensor(out=o_, in0=a_, in1=b_, op=ALU.mult),
                                  ["sg%d" % par, "av%d" % par], [hkey])
                        else:
                            TT(hb[:, j, 0:no], sg[par][:, 0:no], av[par][:, 0:no], ALU.mult, ["sg%d" % par, "av%d" % par], [hkey])
                    else:
                        tc_, cx, yy = ph["bufs"]
                        base = V_SC
                        wc = lambda t: vec[:, base + blk * 3 + t:base + blk * 3 + t + 1]
                        bc = vec[:, base + 24 + blk:base + 24 + blk + 1]
                        ACT(tc_[par][:, 0:n], banks[bks[1]][:, 0:n], AF.Copy, ["B%d" % bks[1]], ["tc%d" % par])
                        TT(cx[par][:, 0:n], tc_[par][:, 0:n], banks[bks[2]][:, 0:n], ALU.mult, ["tc%d" % par, "B%d" % bks[2]], ["cx%d" % par])
                        ACT(yy[par][:, 0:no], cx[par][:, 2:n], AF.Identity, ["cx%d" % par, "vec"], ["yy%d" % par], scale=wc(2), bias=bc)
                        if USE_POOL_SCTAP:
                            for t_, sh in ((1, 1), (0, 0)):
                                src_ = cx[par][:, sh:sh + no]
                                tmp_ = tc_[par][:, 0:no]
                                y_ = yy[par][:, 0:no]
                                w_ = wc(t_)
                                P.add("pool", lambda e, src_=src_, tmp_=tmp_, w_=w_: e.tensor_scalar(
                                    out=tmp_, in0=src_, scalar1=w_, scalar2=1.0, op0=ALU.mult, op1=ALU.mult),
                                    ["cx%d" % par, "vec"], ["tc%d" % par])
                                P.add("pool", lambda e, y_=y_, tmp_=tmp_: e.tensor_tensor(out=y_, in0=y_, in1=tmp_, op=ALU.add),
                                      ["yy%d" % par, "tc%d" % par], ["yy%d" % par])
                        else:
                            STT(yy[par][:, 0:no], cx[par][:, 1:n - 1], wc(1), yy[par][:, 0:no], ALU.mult, ALU.add,
                                ["cx%d" % par, "vec", "yy%d" % par], ["yy%d" % par])
                            STT(yy[par][:, 0:no], cx[par][:, 0:n - 2], wc(0), yy[par][:, 0:no], ALU.mult, ALU.add,
                                ["cx%d" % par, "vec", "yy%d" % par], ["yy%d" % par])
                        TT(hb[:, j, 0:no], yy[par][:, 0:no], banks[bks[0]][:, 2:n], ALU.mult, ["yy%d" % par, "B%d" % bks[0]], [hkey])

            def down(gidx, k, dcs=range(8)):
                P.lbl = "down(%s%d,%d,%d)" % (kind, layer, gidx, k)
                w = W[gidx]
                o0, o1 = OUT_RANGES[k]
                no = o1 - o0
                hb = H[(gidx * 5 + k) % 2]
                hkey = "H%d" % ((gidx * 5 + k) % 2)
                for dc in dcs:
                    bk = dnb[dcnt[0] % len(dnb)]
                    dcnt[0] += 1
                    for j in range(w["nb"]):
                        MM(banks[bk][:, 0:no], w["down"][:, j, dc * 128:(dc + 1) * 128], hb[:, j, 0:no],
                           j == 0, j == w["nb"] - 1, w["keys"] + [hkey], ["B%d" % bk])
                    TT(X[:, dc, o0:o1], X[:, dc, o0:o1], banks[bk][:, 0:no], ALU.add,
                       ["B%d" % bk] + xkeys(o0, o1), xkeys(o0, o1))

            seq = [(g, k) for g in range(ng) for k in range(5)]
            SPLIT_DOWN = os.environ.get("K_SPLITDOWN", "1") == "1"
            for idx, (g, k) in enumerate(seq):
                if idx > 0 and SPLIT_DOWN:
                    pg, pk_ = seq[idx - 1]
                    up(g, k, mid=lambda pg=pg, pk_=pk_: down(pg, pk_, range(0, 4)))
                    down(pg, pk_, range(4, 8))
                else:
                    up(g, k)
                    if idx > 0:
                        down(*seq[idx - 1])
                if k == 0 and g + 1 < ng and (g + 1) not in W:
                    W[g + 1] = ph["load_w"](g + 1)
            down(*seq[-1])
            slot_counter[0] += ng

        final_keys = ["yout"] + (["dbgout"] if (DBG and upto == 1) else [])
        if upto >= 2:
            ph = proj_phase("ffn", 0, V_G + 8, True)
            run_proj(ph)
            TS(X[:, :, 120:128], X[:, :, 120:128], vec[:, V_FLAG:V_FLAG + 1], ALU.mult, ["X0", "vec"], ["X0"])
        if upto >= 3:
            ph = proj_phase("sconv", 1, V_G + 16, False)
            run_proj(ph)
        if upto >= 4:
            ph = proj_phase("ffn", 1, V_G + 24, False)
            run_proj(ph)
        if upto >= 5:
            P.set_fence()
            bF.reset()
            bB.reset()
            sqs3 = [bB.get(512), bB.get(512)]
            lnv3 = bF.get(512)
            rstd3 = bF.get(512)
            for k in range(4):
                c0 = 128 + 512 * k
                norm(lambda kc, c0=c0: X[:, kc, c0:c0 + 512], 512, V_G + 32,
                     lambda kc, c0=c0: X[:, kc, c0:c0 + 512], xkeys(c0, c0 + 512), xkeys(c0, c0 + 512),
                     banks[k % 4], "B%d" % (k % 4), sqs3, lnv3, rstd3, rstd_psum=RSTD_PSUM)
        dump_x()
        P.emit(final_waits=final_keys)
    nc._knames = P.names
    return nc


def make_consts():
    s = np.arange(128)
    tri = (s[:, None] <= s[None, :]).astype(np.float32)
    cf = np.concatenate([tri, np.ones((128, 128), np.float32)], axis=1)
    cb = np.concatenate([np.eye(128, dtype=np.float32)] + [tri] * 4, axis=1)
    return np.ascontiguousarray(cf), np.ascontiguousarray(cb)


def make_vec(inp, half):
    v = np.zeros((128, NV), np.float32)

    def fm(a, nblk):
        return np.asarray(a, np.float32).reshape(nblk, 128).T

    for i, nm in enumerate(["l0_norm_mix", "l0_norm_ffn", "l1_norm_mix", "l1_norm_ffn", "final_norm"]):
        v[:, V_G + 8 * i:V_G + 8 * i + 8] = fm(inp[nm], 8)
    for l in range(2):
        base = V_FC + l * 176
        cw = np.asarray(inp["l0_ffn_conv_w"] if l == 0 else inp["l1_ffn_conv_w"], np.float32)
        v[:, base:base + 132] = cw.reshape(3, 44, 128).transpose(2, 1, 0).reshape(128, 132)
        v[:, base + 132:base + 176] = fm(inp["l0_ffn_conv_b"] if l == 0 else inp["l1_ffn_conv_b"], 44)
    cw = np.asarray(inp["l1_sconv_conv_w"], np.float32)
    v[:, V_SC:V_SC + 24] = cw.reshape(3, 8, 128).transpose(2, 1, 0).reshape(128, 24)
    v[:, V_SC + 24:V_SC + 32] = fm(inp["l1_sconv_conv_b"], 8)
    v[:, V_HN:V_HN + 8] = fm(np.asarray(inp["l0_mlstm_head_norm"], np.float32).reshape(-1), 8)
    v[:, V_BG:V_BG + 8] = np.asarray(inp["l0_mlstm_b_gates"], np.float32)[None, :]
    v[:, V_FLAG] = float(half)
    return v


_CACHE = {}


def kernel(**inputs):
    upto = 5
    if "nc" not in _CACHE:
        _CACHE["nc"] = build_program(upto)
    nc = _CACHE["nc"]
    x = np.asarray(inputs["x"], np.float32)
    cf, cb = make_consts()
    wnames = ["l0_w_in", "l0_w_out", "l0_w_up", "l0_w_down", "l1_w_in", "l1_w_out", "l1_w_up", "l1_w_down"]
    src = {"l0_w_in": "l0_mlstm_w_in", "l0_w_out": "l0_mlstm_w_out", "l0_w_up": "l0_ffn_w_up", "l0_w_down": "l0_ffn_w_down",
           "l1_w_in": "l1_sconv_w_in", "l1_w_out": "l1_sconv_w_out", "l1_w_up": "l1_ffn_w_up", "l1_w_down": "l1_ffn_w_down"}
    wts = {k: np.ascontiguousarray(np.asarray(inputs[src[k]], np.float32)) for k in wnames}
    in_maps = []
    for c in range(NCORES):
        b, half = c // 2, c % 2
        xt = x[b].T
        xm = np.zeros((D, XW), np.float32)
        xp = np.zeros((D, NPRE), np.float32)
        if half == 0:
            xm[:, HALO:] = xt[:, 0:NTOK]
        else:
            xm[:, :] = xt[:, NTOK - HALO:2 * NTOK]
            xp[:, :] = xt[:, 0:NPRE]
        m = dict(xm=np.ascontiguousarray(xm), xp=np.ascontiguousarray(xp), vec=make_vec(inputs, half), cf=cf, cb=cb)
        m.update(wts)
        in_maps.append(m)
    res = run_bass_kernel_spmd(nc, in_maps, core_ids=list(range(NCORES)))
    out = np.empty((4, 2 * NTOK, D), np.float32)
    for c in range(NCORES):
        b, half = c // 2, c % 2
        out[b, half * NTOK:(half + 1) * NTOK, :] = np.asarray(res.results[c]["y"]).T
    return out
```
